# Optimizing a Trainium2 kernel written in Bass

```python
import math
import jax, jax.numpy as jnp
from jax import lax
import numpy as np

D_MODEL = 1024
BATCH = 8
SEQ = 2048
DEPTH = 4

CONV_A_WIDTH = 512
CONV_A_K = 3
CONV_B_WIDTH = 512
CONV_B_K = 31
N_HEADS = 8
N_KV = 2
HEAD_DIM = 64
GROUP = N_HEADS // N_KV
ATTN_WIDTH = N_HEADS * HEAD_DIM
KV_WIDTH = N_KV * HEAD_DIM
N_BRANCH = 3
CMP_BLOCK = 32
CMP_STRIDE = 16
CMP_HIDDEN = 256
SLC_BLOCK = 64
SLC_TOPN = 16
SLC_QCHUNK = 64
WINDOW = 512
WIN_QBLOCK = 128
ROPE_THETA = 10000.0
D_FF = -(-8 * D_MODEL // (3 * 256)) * 256
EPS = 1e-6
NEG = -1e30
IN_SPLITS = [CONV_A_WIDTH] * 3 + [2 * CONV_B_WIDTH] + [ATTN_WIDTH] + [KV_WIDTH] * 6 + [N_HEADS * N_BRANCH] + [D_MODEL] * 3
IN_WIDTH = sum(IN_SPLITS)

kernel_name = "hybrid_conv_conformer_nsa_block"


def rms_norm(x, g):
    xf = x.astype(jnp.float32)
    y = xf * lax.rsqrt(jnp.mean(xf * xf, axis=-1, keepdims=True) + EPS)
    return (y * g.astype(jnp.float32)).astype(x.dtype)


def layer_norm(x, g, b):
    xf = x.astype(jnp.float32)
    mu = jnp.mean(xf, axis=-1, keepdims=True)
    var = jnp.mean(jnp.square(xf - mu), axis=-1, keepdims=True)
    y = (xf - mu) * lax.rsqrt(var + EPS)
    return (y * g.astype(jnp.float32) + b.astype(jnp.float32)).astype(x.dtype)


def depthwise_causal_conv(x, w):
    k = w.shape[0]
    return lax.conv_general_dilated(
        x, w.astype(x.dtype)[:, None, :], window_strides=(1,), padding=[(k - 1, 0)],
        dimension_numbers=('NWC', 'WIO', 'NWC'), feature_group_count=x.shape[-1])


def rope_tables(s):
    pos = jnp.arange(s, dtype=jnp.float32)
    inv = 1.0 / (ROPE_THETA ** (jnp.arange(0, HEAD_DIM, 2, dtype=jnp.float32) / HEAD_DIM))
    ang = pos[:, None] * inv[None, :]
    return jnp.cos(ang), jnp.sin(ang)


def apply_rope(x, cos, sin):
    xf = x.astype(jnp.float32)
    half = x.shape[-1] // 2
    x1, x2 = xf[..., :half], xf[..., half:]
    return jnp.concatenate([x1 * cos - x2 * sin, x2 * cos + x1 * sin], axis=-1).astype(x.dtype)


def masked_softmax(s, mask):
    s = jnp.where(mask, s.astype(jnp.float32), NEG)
    p = jax.nn.softmax(s, axis=-1)
    return jnp.where(mask, p, 0.0)


def short_conv_mixer(c_gate, b_gate, h, conv_w, w_out):
    v = depthwise_causal_conv(c_gate * h, conv_w)
    return (b_gate * v) @ w_out


def conformer_conv_mixer(z, conv_w, conv_b, ln_g, ln_b, w_out):
    a, g = jnp.split(z, 2, axis=-1)
    u = a * jax.nn.sigmoid(g)
    u = depthwise_causal_conv(u, conv_w) + conv_b
    u = jax.nn.silu(layer_norm(u, ln_g, ln_b))
    return u @ w_out


def nsa_mixer(q, kvs, gate_logits, q_g, k_g, cmp_pe, cmp_w1, cmp_b1, cmp_w2, w_out, cos, sin):
    b, s, _ = q.shape
    dt = q.dtype
    scale = HEAD_DIM ** -0.5
    t = jnp.arange(s)
    q = q.reshape(b, s, N_KV, GROUP, HEAD_DIM).transpose(0, 2, 3, 1, 4)
    q = apply_rope(rms_norm(q, q_g), cos, sin)

    def heads(u):
        return u.reshape(b, s, N_KV, HEAD_DIM).transpose(0, 2, 1, 3)

    k_cmp, v_cmp, k_slc, v_slc, k_win, v_win = [heads(u) for u in kvs]
    k_cmp = apply_rope(rms_norm(k_cmp, k_g[0]), cos, sin)
    k_slc = apply_rope(rms_norm(k_slc, k_g[1]), cos, sin)
    k_win = apply_rope(rms_norm(k_win, k_g[2]), cos, sin)

    n_cmp = (s - CMP_BLOCK) // CMP_STRIDE + 1
    blk_idx = jnp.arange(n_cmp)[:, None] * CMP_STRIDE + jnp.arange(CMP_BLOCK)[None, :]

    def compress(u, pe, w1, b1, w2):
        blocks = u[:, :, blk_idx] + pe
        flat = blocks.reshape(b, N_KV, n_cmp, CMP_BLOCK * HEAD_DIM)
        return jax.nn.gelu(flat @ w1 + b1) @ w2

    kc = compress(k_cmp, cmp_pe[0], cmp_w1[0], cmp_b1[0], cmp_w2[0])
    vc = compress(v_cmp, cmp_pe[1], cmp_w1[1], cmp_b1[1], cmp_w2[1])
    block_end = jnp.arange(n_cmp) * CMP_STRIDE + CMP_BLOCK - 1
    cmask = block_end[None, :] <= t[:, None]
    sc = jnp.einsum('bgrsd,bgcd->bgrsc', q, kc) * scale
    p_cmp = masked_softmax(sc, cmask)
    o_cmp = jnp.einsum('bgrsc,bgcd->bgrsd', p_cmp.astype(dt), vc)

    n_sel = s // SLC_BLOCK
    n_top = min(SLC_TOPN, n_sel)
    cmp_start = jnp.arange(n_cmp) * CMP_STRIDE
    sel_start = jnp.arange(n_sel) * SLC_BLOCK
    overlap = ((cmp_start[:, None] < sel_start[None, :] + SLC_BLOCK)
               & (cmp_start[:, None] + CMP_BLOCK > sel_start[None, :])).astype(jnp.float32)
    importance = jnp.einsum('bgrsc,cj->bgsj', p_cmp, overlap)
    cur = t // SLC_BLOCK
    j = jnp.arange(n_sel)
    valid = j[None, :] <= cur[:, None]
    forced = (j[None, :] == 0) | (j[None, :] == cur[:, None]) | (j[None, :] == cur[:, None] - 1)
    score = jnp.where(forced, jnp.inf, jnp.where(valid, importance, -jnp.inf))
    top_val, top_idx = lax.top_k(score, n_top)
    top_ok = top_val > -jnp.inf

    n_q = s // SLC_QCHUNK
    kb = k_slc.reshape(b, N_KV, n_sel, SLC_BLOCK, HEAD_DIM)
    vb = v_slc.reshape(b, N_KV, n_sel, SLC_BLOCK, HEAD_DIM)
    q_ch = q.reshape(b, N_KV, GROUP, n_q, SLC_QCHUNK, HEAD_DIM).transpose(3, 0, 1, 2, 4, 5)
    idx_ch = top_idx.reshape(b, N_KV, n_q, SLC_QCHUNK, n_top).transpose(2, 0, 1, 3, 4)
    ok_ch = top_ok.reshape(b, N_KV, n_q, SLC_QCHUNK, n_top).transpose(2, 0, 1, 3, 4)
    t_ch = t.reshape(n_q, SLC_QCHUNK)
    gather = jax.vmap(jax.vmap(lambda blocks, ix: blocks[ix]))
    n_keys = n_top * SLC_BLOCK

    def slc_chunk(args):
        qc, ic, okc, tc = args
        kg = gather(kb, ic).reshape(b, N_KV, SLC_QCHUNK, n_keys, HEAD_DIM)
        vg = gather(vb, ic).reshape(b, N_KV, SLC_QCHUNK, n_keys, HEAD_DIM)
        kpos = (ic[..., None] * SLC_BLOCK + jnp.arange(SLC_BLOCK)).reshape(b, N_KV, SLC_QCHUNK, n_keys)
        m = jnp.repeat(okc, SLC_BLOCK, axis=-1) & (kpos <= tc[None, None, :, None])
        ss = jnp.einsum('bgrqd,bgqkd->bgrqk', qc, kg) * scale
        pp = masked_softmax(ss, m[:, :, None])
        return jnp.einsum('bgrqk,bgqkd->bgrqd', pp.astype(dt), vg)

    o_slc = lax.map(slc_chunk, (q_ch, idx_ch, ok_ch, t_ch))
    o_slc = o_slc.transpose(1, 2, 3, 0, 4, 5).reshape(b, N_KV, GROUP, s, HEAD_DIM)

    n_b = s // WIN_QBLOCK
    band = WIN_QBLOCK + WINDOW
    band_idx = jnp.arange(n_b)[:, None] * WIN_QBLOCK + jnp.arange(band)[None, :]
    pad = ((0, 0), (0, 0), (WINDOW, 0), (0, 0))
    kw = jnp.pad(k_win, pad)[:, :, band_idx]
    vw = jnp.pad(v_win, pad)[:, :, band_idx]
    kpos = band_idx - WINDOW
    tq = t.reshape(n_b, WIN_QBLOCK)
    wmask = ((kpos[:, None, :] >= 0) & (kpos[:, None, :] <= tq[:, :, None])
             & (kpos[:, None, :] > tq[:, :, None] - WINDOW))
    qw = q.reshape(b, N_KV, GROUP, n_b, WIN_QBLOCK, HEAD_DIM)
    sw = jnp.einsum('bgrnqd,bgnkd->bgrnqk', qw, kw) * scale
    pw = masked_softmax(sw, wmask)
    o_win = jnp.einsum('bgrnqk,bgnkd->bgrnqd', pw.astype(dt), vw).reshape(b, N_KV, GROUP, s, HEAD_DIM)

    g = jax.nn.sigmoid(gate_logits).reshape(b, s, N_KV, GROUP, N_BRANCH).transpose(0, 2, 3, 1, 4)
    o = g[..., 0:1] * o_cmp + g[..., 1:2] * o_slc + g[..., 2:3] * o_win
    o = o.astype(dt).transpose(0, 3, 1, 2, 4).reshape(b, s, ATTN_WIDTH)
    return o @ w_out


def setup_inputs(seed: int = 0) -> dict:
    key = jax.random.key(seed)
    ks = jax.random.split(key, 24)
    L = DEPTH

    def nrm(k, shape, scale):
        return jax.random.normal(k, shape, jnp.float32) * scale

    return {
        'x': nrm(ks[0], (BATCH, SEQ, D_MODEL), 1.0),
        'norm1_g': 1.0 + nrm(ks[1], (L, D_MODEL), 0.05),
        'w_in': nrm(ks[2], (L, D_MODEL, IN_WIDTH), D_MODEL ** -0.5),
        'a_conv_w': nrm(ks[3], (L, CONV_A_K, CONV_A_WIDTH), CONV_A_K ** -0.5),
        'a_w_out': nrm(ks[4], (L, CONV_A_WIDTH, D_MODEL), CONV_A_WIDTH ** -0.5),
        'b_conv_w': nrm(ks[5], (L, CONV_B_K, CONV_B_WIDTH), CONV_B_K ** -0.5),
        'b_conv_b': nrm(ks[6], (L, CONV_B_WIDTH), 0.02),
        'b_ln_g': 1.0 + nrm(ks[7], (L, CONV_B_WIDTH), 0.05),
        'b_ln_b': nrm(ks[8], (L, CONV_B_WIDTH), 0.02),
        'b_w_out': nrm(ks[9], (L, CONV_B_WIDTH, D_MODEL), CONV_B_WIDTH ** -0.5),
        'q_norm_g': 1.0 + nrm(ks[10], (L, HEAD_DIM), 0.05),
        'k_norm_g': 1.0 + nrm(ks[11], (L, N_BRANCH, HEAD_DIM), 0.05),
        'cmp_pe': nrm(ks[12], (L, 2, CMP_BLOCK, HEAD_DIM), 0.02),
        'cmp_w1': nrm(ks[13], (L, 2, CMP_BLOCK * HEAD_DIM, CMP_HIDDEN), (CMP_BLOCK * HEAD_DIM) ** -0.5),
        'cmp_b1': nrm(ks[14], (L, 2, CMP_HIDDEN), 0.02),
        'cmp_w2': nrm(ks[15], (L, 2, CMP_HIDDEN, HEAD_DIM), CMP_HIDDEN ** -0.5),
        'nsa_w_out': nrm(ks[16], (L, ATTN_WIDTH, D_MODEL), ATTN_WIDTH ** -0.5),
        'w_o': nrm(ks[17], (L, D_MODEL, D_MODEL), 0.5 * D_MODEL ** -0.5),
        'norm2_g': 1.0 + nrm(ks[18], (L, D_MODEL), 0.05),
        'ffn_w13': nrm(ks[19], (L, D_MODEL, 2 * D_FF), D_MODEL ** -0.5),
        'ffn_w2': nrm(ks[20], (L, D_FF, D_MODEL), 0.5 * D_FF ** -0.5),
    }


def reference(x, norm1_g, w_in, a_conv_w, a_w_out, b_conv_w, b_conv_b, b_ln_g, b_ln_b, b_w_out,
              q_norm_g, k_norm_g, cmp_pe, cmp_w1, cmp_b1, cmp_w2, nsa_w_out, w_o, norm2_g,
              ffn_w13, ffn_w2):
    s = x.shape[1]
    cos, sin = rope_tables(s)
    offsets = [sum(IN_SPLITS[:i + 1]) for i in range(len(IN_SPLITS) - 1)]
    for l in range(DEPTH):
        h = rms_norm(x, norm1_g[l])
        parts = jnp.split(h @ w_in[l], offsets, axis=-1)
        a_c, a_b, a_h, b_z, q = parts[0], parts[1], parts[2], parts[3], parts[4]
        kvs = parts[5:11]
        nsa_gates = parts[11]
        g_a, g_b, g_c = [jax.nn.sigmoid(p) for p in parts[12:15]]
        y_a = short_conv_mixer(a_c, a_b, a_h, a_conv_w[l], a_w_out[l])
        y_b = conformer_conv_mixer(b_z, b_conv_w[l], b_conv_b[l], b_ln_g[l], b_ln_b[l], b_w_out[l])
        y_c = nsa_mixer(q, kvs, nsa_gates, q_norm_g[l], k_norm_g[l], cmp_pe[l], cmp_w1[l], cmp_b1[l],
                        cmp_w2[l], nsa_w_out[l], cos, sin)
        mixed = g_a * y_a + g_b * y_b + g_c * y_c
        x = x + (mixed @ w_o[l]).astype(x.dtype)
        h2 = rms_norm(x, norm2_g[l])
        u, v = jnp.split(h2 @ ffn_w13[l], 2, axis=-1)
        x = x + ((jax.nn.silu(u) * v) @ ffn_w2[l]).astype(x.dtype)
    return x
```

```python
import contextlib
import itertools
import numpy as np
import concourse.bass as bass
import concourse.mybir as mybir
from concourse.bass_utils import run_bass_kernel_spmd

F32 = mybir.dt.float32
BF16 = mybir.dt.bfloat16
AF = mybir.ActivationFunctionType
ALU = mybir.AluOpType
AX = mybir.AxisListType

SAME_ENGINE_SYNC = False
DMA_SLOTS = 8

L = 4
S = 2048
D = 1024
NT = 16
EPS = 1e-6
INW = 6936
C_AC, C_AB, C_AH = 0, 512, 1024
C_BA, C_BG = 1536, 2048
C_Q = 2560
C_KC, C_VC, C_KS, C_VS, C_KW, C_VW = 3072, 3200, 3328, 3456, 3584, 3712
C_GT = 3840
C_GA, C_GB, C_GC = 3864, 4888, 5912
P_N1, P_N2, P_ACW, P_BCW, P_BCB, P_BLG, P_BLB, P_QG, P_KG, P_B1, P_PE = 0, 8, 16, 28, 152, 156, 160, 164, 165, 168, 172
NPC = 204
K_ID, K_ROT, K_B64, K_O1024, K_O512, K_ONE, K_DIAG, K_LOW, K_OV = 0, 128, 256, 384, 512, 640, 768, 1280, 1792
NCB = 1824
BIGSEL = 30000.0


class Reg:
    __slots__ = ("name", "lw", "rd", "excl")

    def __init__(self, name, excl=False):
        self.name = name
        self.lw = None
        self.rd = []
        self.excl = excl


class V:
    __slots__ = ("ap", "regs")

    def __init__(self, ap, regs):
        self.ap = ap
        self.regs = regs if isinstance(regs, (list, tuple)) else [regs]


class TT:
    def __init__(self, t, name, g=0, excl=False):
        self.t = t
        self.name = name
        self.g = g
        self.regs = {}
        self.excl = excl

    def reg(self, idx):
        r = self.regs.get(idx)
        if r is None:
            r = self.regs[idx] = Reg(f"{self.name}{list(idx)}", self.excl)
        return r

    def regs_for(self, key):
        gi = key[1:1 + self.g]
        shape = self.t.shape
        rngs = []
        for d in range(self.g):
            i = gi[d] if d < len(gi) else slice(None)
            if isinstance(i, int):
                rngs.append([i])
            else:
                rngs.append(list(range(*i.indices(shape[1 + d]))))
        return [self.reg(tuple(ix)) for ix in itertools.product(*rngs)]

    def __getitem__(self, key):
        if not isinstance(key, tuple):
            key = (key,)
        return V(self.t[key], self.regs_for(key))

    def allregs(self):
        return self.regs_for((slice(None),))


class Op:
    __slots__ = ("waits", "fn", "signal", "dma")

    def __init__(self, fn):
        self.waits = []
        self.fn = fn
        self.signal = False
        self.dma = None


class Plan:
    ENG = ["pe", "act", "dve", "pool", "sp"]

    def __init__(self, nc):
        self.nc = nc
        self.ops = {e: [] for e in self.ENG}
        self.known = {e: {} for e in self.ENG}
        self.ndma = {e: 0 for e in self.ENG}
        self.tok = Reg("phase_token")

    def _need(self, eng, ev, op, raw=False):
        if ev is None:
            return
        if ev[0] == 'c':
            _, src, idx = ev
            if src == eng and eng == "pe":
                return
            k = self.known[eng]
            if k.get(src, -1) >= idx:
                return
            k[src] = idx
            self.ops[src][idx].signal = True
            op.waits.append(ev)
        else:
            _, q, slot, val = ev
            k = self.known[eng]
            if k.get((q, slot), 0) >= val:
                return
            k[(q, slot)] = val
            op.waits.append(ev)

    def add(self, eng, fn, reads=(), writes=(), dma=False):
        op = Op(fn)
        idx = len(self.ops[eng])
        rregs = []
        wregs = []
        for v in reads:
            if v is not None:
                for r in v.regs:
                    (wregs if r.excl else rregs).append(r)
        for v in writes:
            if v is not None:
                wregs.extend(v.regs)
        cand = {}

        def consider(ev, raw):
            if ev is None:
                return
            if ev[0] == 'c':
                if ev[1] == eng and eng == "pe":
                    return
                key = ('c', ev[1])
                if key not in cand or cand[key][2] < ev[2]:
                    cand[key] = ev
            else:
                key = ('d', ev[1], ev[2])
                if key not in cand or cand[key][3] < ev[3]:
                    cand[key] = ev
        for r in rregs:
            consider(r.lw, True)
        for r in wregs:
            consider(r.lw, False)
            for ev in r.rd:
                consider(ev, False)
        for ev in cand.values():
            self._need(eng, ev, op, raw=True)
        if dma:
            j = self.ndma[eng]
            self.ndma[eng] = j + 1
            slot = j % DMA_SLOTS
            val = 16 * (j // DMA_SLOTS + 1)
            if j >= DMA_SLOTS:
                self._need(eng, ('d', eng, slot, val - 16), op)
            op.dma = (slot, val)
            ev = ('d', eng, slot, val)
        else:
            ev = ('c', eng, idx)
        for r in rregs:
            r.rd.append(ev)
        for r in wregs:
            r.lw = ev
            r.rd = []
        self.ops[eng].append(op)
        return op

    def barrier(self):
        last = {}
        BENG = [e for e in self.ENG if e != "pool"]
        for e in self.ENG:
            n = len(self.ops[e])
            if n:
                for i in range(n - 1, -1, -1):
                    if self.ops[e][i].dma is None:
                        last[e] = ('c', e, i)
                        break
        dl = []
        for q in BENG:
            n = self.ndma[q]
            for s in range(min(DMA_SLOTS, n)):
                nd = (n - s + DMA_SLOTS - 1) // DMA_SLOTS
                dl.append(('d', q, s, 16 * nd))
        for e in ("act", "dve", "sp"):
            op = Op(lambda eng: eng.nop())
            for src, ev in last.items():
                self._need(e, ev, op)
            for ev in dl:
                self._need(e, ev, op)
            if op.waits:
                self.ops[e].append(op)
                if e == "dve":
                    self.tok.lw = ('c', 'dve', len(self.ops[e]) - 1)
                    self.tok.rd = []

    def emit(self):
        nc = self.nc
        cnt = {}
        for e in self.ENG:
            c = 0
            for i, op in enumerate(self.ops[e]):
                if op.signal and op.dma is None:
                    c += 1
                    cnt[(e, i)] = c
        with contextlib.ExitStack() as st:
            csem = {e: st.enter_context(nc.semaphore(f"c_{e}")) for e in self.ENG}
            dsem = {}
            for e in self.ENG:
                for s in range(min(DMA_SLOTS, self.ndma[e])):
                    dsem[(e, s)] = st.enter_context(nc.semaphore(f"d_{e}{s}"))
            block = st.enter_context(nc.Block())

            def replay(e):
                def body(eng):
                    for i, op in enumerate(self.ops[e]):
                        for ev in op.waits:
                            if ev[0] == 'c':
                                eng.wait_ge(csem[ev[1]], cnt[(ev[1], ev[2])])
                            else:
                                eng.wait_ge(dsem[(ev[1], ev[2])], ev[3])
                        ins = op.fn(eng)
                        if op.dma is not None:
                            ins.then_inc(dsem[(e, op.dma[0])], 16)
                        elif op.signal:
                            ins.then_inc(csem[e], 1)
                    if e == "sp":
                        for q in self.ENG:
                            n = self.ndma[q]
                            for s in range(min(DMA_SLOTS, n)):
                                nd = (n - s + DMA_SLOTS - 1) // DMA_SLOTS
                                eng.wait_ge(dsem[(q, s)], 16 * nd)
                return body

            block.tensor(replay("pe"))
            block.scalar(replay("act"))
            block.vector(replay("dve"))
            block.gpsimd(replay("pool"))
            block.sync(replay("sp"))

    def stats(self):
        return {e: (len(self.ops[e]), sum(len(o.waits) for o in self.ops[e])) for e in self.ENG}

    def mm(self, out, lhsT, rhs, start=True, stop=True):
        return self.add("pe", lambda e: e.matmul(out.ap, lhsT.ap, rhs.ap, start=start, stop=stop),
                        reads=[lhsT, rhs], writes=[out])

    def transpose(self, out, in_, ident):
        return self.add("pe", lambda e: e.transpose(out.ap, in_.ap, ident.ap),
                        reads=[in_, ident], writes=[out])

    def act(self, out, in_, func, bias=None, scale=None, eng="act"):
        def fn(e):
            kw = {}
            if bias is not None:
                kw["bias"] = bias.ap if isinstance(bias, V) else bias
            if scale is not None:
                kw["scale"] = scale.ap if isinstance(scale, V) else scale
            return e.activation(out.ap, in_.ap, func, **kw)
        rd = [in_] + [x for x in (bias, scale) if isinstance(x, V)]
        return self.add(eng, fn, reads=rd, writes=[out])

    def tt(self, out, in0, in1, op, eng="dve"):
        return self.add(eng, lambda e: e.tensor_tensor(out.ap, in0.ap, in1.ap, op),
                        reads=[in0, in1], writes=[out])

    def ts(self, out, in0, s1, op0, s2=None, op1=None, eng="dve"):
        def fn(e):
            a1 = s1.ap if isinstance(s1, V) else s1
            a2 = s2.ap if isinstance(s2, V) else s2
            kw = {}
            if op1 is not None:
                kw["op1"] = op1
            return e.tensor_scalar(out.ap, in0.ap, a1, a2, op0, **kw)
        rd = [in0] + [x for x in (s1, s2) if isinstance(x, V)]
        return self.add(eng, fn, reads=rd, writes=[out])

    def stt(self, out, in0, scalar, in1, op0, op1, eng="dve"):
        def fn(e):
            a = scalar.ap if isinstance(scalar, V) else scalar
            return e.scalar_tensor_tensor(out.ap, in0.ap, a, in1.ap, op0, op1)
        rd = [in0, in1] + ([scalar] if isinstance(scalar, V) else [])
        return self.add(eng, fn, reads=rd, writes=[out])

    def copy(self, out, in_, eng="dve"):
        if eng == "act":
            return self.add("act", lambda e: e.copy(out.ap, in_.ap), reads=[in_], writes=[out])
        return self.add(eng, lambda e: e.tensor_copy(out.ap, in_.ap), reads=[in_], writes=[out])

    def memset(self, out, val, eng="dve"):
        return self.add(eng, lambda e: e.memset(out.ap, val), writes=[out])

    def dma(self, out, in_, q="sp", **kw):
        return self.add(q, lambda e: e.dma_start(out.ap, in_.ap, **kw), reads=[in_], writes=[out], dma=True)


class Ring:
    def __init__(self, tt, n):
        self.tt = tt
        self.n = n
        self.i = 0

    def next(self):
        k = self.i % self.n
        self.i += 1
        return k


def bc_ap(v, pattern):
    return V(bass.AP(v.ap.tensor, v.ap.offset, pattern), v.regs)


def _consts():
    cb = np.zeros((128, NCB), np.float32)
    cb[:, K_ID:K_ID + 128] = np.eye(128, dtype=np.float32)
    rot = np.zeros((128, 128), np.float32)
    for m in range(128):
        if m % 64 < 32:
            rot[m + 32, m] = -1.0
        else:
            rot[m - 32, m] = 1.0
    cb[:, K_ROT:K_ROT + 128] = rot
    b64 = np.zeros((128, 128), np.float32)
    b64[:64, :64] = 1.0 / 64
    b64[64:, 64:] = 1.0 / 64
    cb[:, K_B64:K_B64 + 128] = b64
    cb[:, K_O1024:K_O1024 + 128] = 1.0 / 1024
    cb[:, K_O512:K_O512 + 128] = 1.0 / 512
    cb[:, K_ONE:K_ONE + 128] = 1.0
    kk = np.arange(128)[:, None]
    qq = np.arange(128)[None, :]
    cb[:, K_DIAG:K_DIAG + 512] = np.tile((kk <= qq).astype(np.float32), (1, 4))
    cb[:, K_LOW:K_LOW + 512] = np.tile((kk > qq).astype(np.float32), (1, 4))
    c = np.arange(128)[:, None]
    j = np.arange(32)[None, :]
    ov = ((16 * c < 64 * j + 64) & (16 * c + 32 > 64 * j) & (c < 127)).astype(np.float32)
    cb[:, K_OV:K_OV + 32] = ov
    em = np.zeros((128, 2048), np.float32)
    key = np.arange(2048)[None, :]
    em[64:96] = (key // 64 == np.arange(32)[:, None]).astype(np.float32)
    t = np.arange(2048)[None, :]
    cm = ((16 * c + 31 <= t) & (c < 127)).astype(np.float32)
    pos = np.arange(2048, dtype=np.float32)
    inv = (1.0 / (np.float32(10000.0) ** (np.arange(0, 64, 2, dtype=np.float32) / np.float32(64)))).astype(np.float32)
    ang = pos[:, None] * inv[None, :]
    cos = np.cos(ang).astype(np.float32)
    sin = np.sin(ang).astype(np.float32)
    pidx = (np.arange(128) % 64) % 32
    cosT = np.ascontiguousarray(cos[:, pidx].T)
    sinT = np.ascontiguousarray(sin[:, pidx].T)
    bs = np.zeros((128, 8, 32), np.float32)
    for ii in range(8):
        i = ii + 8
        for p in range(128):
            cur = 2 * i + (1 if p >= 64 else 0)
            for jj in range(32):
                if jj == 0 or jj == cur or jj == cur - 1:
                    bs[p, ii, jj] = 1.0e4
                elif jj <= cur:
                    bs[p, ii, jj] = 0.0
                else:
                    bs[p, ii, jj] = -1.0e4
    cf = np.zeros((128, 128 + 256), np.float32)
    cf[:, 0:128] = np.eye(128, dtype=np.float32)
    cf[:, 128:384] = bs.reshape(128, 256)
    return dict(cst_b=cb, cst_e=em, cst_cm=cm, cst_cos=cosT, cst_sin=sinT, cst_f=cf)


def _params(inp):
    pr = np.zeros((L, 128, NPC), np.float32)
    p = np.arange(128)
    for l in range(L):
        pr[l, :, P_N1:P_N1 + 8] = inp["norm1_g"][l].reshape(8, 128).T
        pr[l, :, P_N2:P_N2 + 8] = inp["norm2_g"][l].reshape(8, 128).T
        acw = inp["a_conv_w"][l]
        for c in range(4):
            pr[l, :, P_ACW + c * 3:P_ACW + c * 3 + 3] = acw[:, c * 128:(c + 1) * 128].T
        bcw = inp["b_conv_w"][l]
        for c in range(4):
            pr[l, :, P_BCW + c * 31:P_BCW + c * 31 + 31] = bcw[:, c * 128:(c + 1) * 128].T
        pr[l, :, P_BCB:P_BCB + 4] = inp["b_conv_b"][l].reshape(4, 128).T
        pr[l, :, P_BLG:P_BLG + 4] = inp["b_ln_g"][l].reshape(4, 128).T
        pr[l, :, P_BLB:P_BLB + 4] = inp["b_ln_b"][l].reshape(4, 128).T
        pr[l, :, P_QG] = inp["q_norm_g"][l][p % 64]
        for b in range(3):
            pr[l, :, P_KG + b] = inp["k_norm_g"][l, b][p % 64]
        for kv in range(2):
            pr[l, :, P_B1 + kv * 2:P_B1 + kv * 2 + 2] = inp["cmp_b1"][l, kv].reshape(2, 128).T
            pe = inp["cmp_pe"][l, kv]
            pe2 = pe.reshape(16, 2, 64).transpose(1, 2, 0).reshape(128, 16)
            pr[l, :, P_PE + kv * 16:P_PE + kv * 16 + 16] = pe2
    return pr


def _prep(inputs):
    inp = {k: np.asarray(v, dtype=np.float32) for k, v in inputs.items()}
    w_in = inp["w_in"].copy()
    q = inp["w_in"][:, :, C_Q:C_Q + 512].reshape(L, 1024, 2, 4, 64)
    w_in[:, :, C_Q:C_Q + 512] = q.transpose(0, 1, 3, 2, 4).reshape(L, 1024, 512)
    shared = dict(
        w_in=np.ascontiguousarray(w_in),
        a_w_out=inp["a_w_out"], b_w_out=inp["b_w_out"], nsa_w_out=inp["nsa_w_out"],
        w_o=inp["w_o"], ffn_w13=inp["ffn_w13"], ffn_w2=inp["ffn_w2"],
        cmp_w1=np.ascontiguousarray(inp["cmp_w1"].reshape(L * 2, 2048, 256)),
        cmp_w2=np.ascontiguousarray(inp["cmp_w2"].reshape(L * 2, 256, 64)),
        params=_params(inp),
    )
    shared.update(_consts())
    xs = [np.ascontiguousarray(inp["x"][b].T) for b in range(8)]
    return shared, xs


SB0 = 16512
SB_END = 229344
O_H = SB0
O_WP = O_H + 32768
O_CB = O_WP + 40960
O_CF = O_CB + 3712
O_PRM = O_CF + 1536
O_PRMB = O_PRM + 3328
O_KE = O_PRMB + 128
O_CM = O_KE + 8192
O_COS = O_CM + 4096
O_SIN = O_COS + 8192
O_W2C = O_SIN + 8192
O_PH = O_W2C + 512
PH_SIZE = SB_END - O_PH
NWSLOT = 5


def build(n_layers=L, taps=(), upto=None, dbg=()):
    nc = bass.Bass("TRN2", target_bir_lowering=False)
    P = Plan(nc)
    taps = set(taps)
    tap_out = {}

    def din(name, shape):
        return nc.dram_tensor(name, shape, F32, kind="ExternalInput")

    xT = din("xT", [D, S])
    yT = nc.dram_tensor("yT", [D, S], F32, kind="ExternalOutput")
    w_in = din("w_in", [L, D, INW])
    a_w_out = din("a_w_out", [L, 512, D])
    b_w_out = din("b_w_out", [L, 512, D])
    nsa_w_out = din("nsa_w_out", [L, 512, D])
    w_o = din("w_o", [L, D, D])
    ffn_w13 = din("ffn_w13", [L, D, 5632])
    ffn_w2 = din("ffn_w2", [L, 2816, D])
    cmp_w1 = din("cmp_w1", [L * 2, 2048, 256])
    cmp_w2 = din("cmp_w2", [L * 2, 256, 64])
    params = din("params", [L, 128, NPC])
    cst_b = din("cst_b", [128, NCB])
    cst_e = din("cst_e", [128, 2048])
    cst_cm = din("cst_cm", [128, 2048])
    cst_cos = din("cst_cos", [128, 2048])
    cst_sin = din("cst_sin", [128, 2048])
    cst_f = din("cst_f", [128, 384])

    def NR(ap):
        return V(ap, [])

    names = itertools.count()

    def sb_at(name, shape, dt, off, g=0):
        size = int(np.prod(shape[1:])) * (2 if dt == BF16 else 4)
        assert off + size <= SB_END, (name, off, size)
        t = nc.alloc_sbuf_tensor_at(f"{name}_{next(names)}", list(shape), dt, offset=off)
        return TT(t, name, g)

    def tap(name, view, shape):
        if name in taps:
            o = nc.dram_tensor("tap_" + name, list(shape), F32, kind="ExternalOutput")
            tap_out[name] = o
            return o
        return None

    H = sb_at("H", [128, 8, 4, 512], BF16, O_H, g=2)
    WSL = [sb_at(f"W{i}", [128, 4096], BF16, O_WP + i * 8192) for i in range(NWSLOT)]
    CB = sb_at("CB", [128, NCB], BF16, O_CB)
    CF = sb_at("CF", [128, 384], F32, O_CF)
    PRM = sb_at("PRM", [128, L, NPC], F32, O_PRM)
    PRMB = sb_at("PRMB", [128, 32], BF16, O_PRMB)
    KE = sb_at("KE", [128, 2, 4, 512], BF16, O_KE, g=2)
    CM = sb_at("CM", [128, 2048], BF16, O_CM)
    COS = sb_at("COS", [128, 4, 512], F32, O_COS)
    SIN = sb_at("SIN", [128, 4, 512], F32, O_SIN)
    W2C = sb_at("W2C", [128, 2, 2, 64], BF16, O_W2C, g=1)
    PSB = [TT(nc.alloc_psum_tensor(f"psb{i}", [128, 512], F32), f"psb{i}", excl=True) for i in range(8)]

    class PsRing:
        def __init__(self, banks):
            self.b = banks
            self.i = 0

        def next(self):
            k = self.b[self.i % len(self.b)]
            self.i += 1
            return PSB[k]

    wslot_i = [0]

    def loadw(dram_ap_2d, nk, ncols):
        slot = WSL[wslot_i[0] % NWSLOT]
        wslot_i[0] += 1
        assert nk * ncols <= 4096
        dst = V(slot.t[:, 0:nk * ncols].rearrange("p (k n) -> p k n", k=nk), slot[:, :].regs)
        src = NR(dram_ap_2d.rearrange("(k p) n -> p k n", p=128))
        P.dma(dst, src, q="pool")

        def acc(kc, c0, c1):
            return V(slot.t[:, kc * ncols + c0: kc * ncols + c1], slot[:, :].regs)
        return acc

    def prm(l, col, n=1):
        return V(PRM.t[:, l, col:col + n], PRM[:, :, :].regs)

    def cb(col, n=128, p0=0, p1=128):
        return V(CB.t[p0:p1, col:col + n], CB[:, :].regs)

    P.dma(CB[:, :], NR(cst_b[:, :]), q="pool")
    P.dma(CF[:, :], NR(cst_f[:, :]), q="sp")
    P.dma(PRM[:, :, :], NR(params.rearrange("l p n -> p l n")), q="sp")
    P.dma(V(COS.t[:, :, :], COS.allregs()), NR(cst_cos.rearrange("p (a b) -> p a b", a=4)), q="sp")
    P.dma(V(SIN.t[:, :, :], SIN.allregs()), NR(cst_sin.rearrange("p (a b) -> p a b", a=4)), q="sp")
    P.dma(CM[:, :], NR(cst_cm[:, :]), q="pool")
    for g in range(2):
        P.dma(V(KE.t[64:96, g, :, :], KE.regs_for((0, g))), NR(cst_e[64:96, :].rearrange("p (a b) -> p a b", a=4)), q="pool")
    P.memset(V(KE.t[96:128, :, :, :], KE.allregs()), 0.0)
    IDF = V(CF.t[:, 0:128], CF[:, :].regs)

    Xreg = {(c, tg): Reg(f"X{c}_{tg}") for c in range(8) for tg in range(4)}
    xstate = {"in_y": False}

    def Xsrc(c, tg):
        if xstate["in_y"]:
            return V(yT[c * 128:(c + 1) * 128, tg * 512:(tg + 1) * 512], [Xreg[(c, tg)]])
        return NR(xT[c * 128:(c + 1) * 128, tg * 512:(tg + 1) * 512])

    def Xdst(c, tg):
        return V(yT[c * 128:(c + 1) * 128, tg * 512:(tg + 1) * 512], [Xreg[(c, tg)]])

    def dump(name, view_fn, shape, nparts=128):
        pass

    def stage_norm(l, gcol):
        o = O_PH
        XT = sb_at("nXT", [128, 16, 512], F32, o, g=1); o += 32768
        SQ = sb_at("nSQ", [128, 4, 512], BF16, o, g=1); o += 4096
        LR = sb_at("nLR", [128, 4, 512], F32, o, g=1); o += 8192
        psr = PsRing([6, 7])
        xi = 0
        for tg in range(4):
            ps = psr.next()
            xs = []
            for c in range(8):
                k = xi % 16; xi += 1
                P.dma(XT[:, k, :], Xsrc(c, tg), q="sp")
                sq = SQ[:, k % 4, :]
                P.act(sq, XT[:, k, :], AF.Square)
                P.mm(ps[:, :], cb(K_O1024), sq, start=(c == 0), stop=(c == 7))
                xs.append(k)
            lk = (tg % 2) * 2
            P.act(LR[:, lk, :], ps[:, :], AF.Ln, bias=EPS)
            P.act(LR[:, lk + 1, :], LR[:, lk, :], AF.Exp, scale=-0.5)
            for c in range(8):
                P.stt(H[:, c, tg, :], XT[:, xs[c], :], prm(l, gcol + c), LR[:, lk + 1, :], ALU.mult, ALU.mult)

    def stage_norm_apply(l, gcol):
        o = O_PH + 16384
        XA = Ring(sb_at("aXT", [128, 10, 512], F32, o, g=1), 10); o += 20480
        RS = sb_at("aRS", [128, 4, 512], F32, o, g=1); o += 8192
        LT = sb_at("aLT", [128, 2, 512], F32, o, g=1); o += 4096
        assert o <= O_MIXED
        its = [(tg, c) for tg in range(4) for c in range(8)]
        PF = 8
        slots = {}

        def xload(n):
            tg, c = its[n]
            k = XA.next()
            slots[n] = k
            P.dma(XA.tt[:, k, :], Xsrc(c, tg), q="sp")
        for n in range(PF):
            xload(n)
        for n, (tg, c) in enumerate(its):
            if c == 0:
                P.act(LT[:, tg % 2, :], PSB[4 + tg][:, :], AF.Ln, bias=EPS)
                P.act(RS[:, tg, :], LT[:, tg % 2, :], AF.Exp, scale=-0.5)
            k = slots[n]
            P.stt(H[:, c, tg, :], XA.tt[:, k, :], prm(l, gcol + c), RS[:, tg, :], ALU.mult, ALU.mult)
            if n + PF < len(its):
                xload(n + PF)

    def proj_fm(ps, wacc, c0, nk, src, tg):
        for kc in range(nk):
            P.mm(ps[:, :], wacc(kc, c0, c0 + 128), src[:, kc, tg, :], start=(kc == 0), stop=(kc == nk - 1))

    def out_branch(l, SRC, src_fn, wout, gcol, first, MIXED, SG, MM, psr):
        for jg in range(2):
            WO = loadw(wout[l, :, jg * 512:(jg + 1) * 512], 4, 512)
            WG = loadw(w_in[l, :, gcol + jg * 512: gcol + (jg + 1) * 512], 8, 512)
            for jj in range(4):
                j = jg * 4 + jj
                for tg in range(4):
                    py = psr.next()
                    for kc in range(4):
                        P.mm(py[:, :], WO(kc, jj * 128, jj * 128 + 128), src_fn(kc, tg), start=(kc == 0), stop=(kc == 3))
                    pg = psr.next()
                    proj_fm(pg, WG, jj * 128, 8, H, tg)
                    k = SG.next()
                    P.act(SG.tt[:, k, :], pg[:, :], AF.Sigmoid)
                    if first:
                        P.tt(MIXED[:, j, tg, :], py[:, :], SG.tt[:, k, :], ALU.mult)
                    else:
                        m = MM.next()
                        P.tt(MM.tt[:, m, :], py[:, :], SG.tt[:, k, :], ALU.mult)
                        P.tt(MIXED[:, j, tg, :], MIXED[:, j, tg, :], MM.tt[:, m, :], ALU.add)

    O_MIXED = O_PH + 49152

    def stage_nsa(l):
        MIXED = sb_at("MIXED", [128, 8, 4, 512], BF16, O_MIXED, g=2)
        o = O_PH
        QT = sb_at("QT", [128, 4, 4, 512], BF16, o, g=2); o += 16384
        KWZ = sb_at("KWZ", [128, 2, 4, 512], BF16, O_PH + 90112, g=2)
        KC = sb_at("KC", [128, 128], BF16, o); o += 256
        VCX = sb_at("VCX", [128, 2, 97], BF16, o); o += 512
        VS = sb_at("VS", [128, 16, 2, 65], BF16, o, g=1); o += 4160
        VW = sb_at("VW", [128, 16, 2, 65], BF16, o, g=1); o += 4160
        GT = sb_at("GT", [128, 16, 24], F32, o, g=1); o += 1536
        CBIAS = sb_at("CBIAS", [128, 4], F32, o); o += 64
        assert o <= O_PH + 32768
        o = O_PH + 32768
        OT = sb_at("OT", [128, 4, 4, 512], BF16, o, g=2); o += 16384
        o1 = o
        assert o1 == O_MIXED
        KCV = sb_at("KCV", [128, 2, 2048], BF16, o, g=1); o += 8192
        U2 = sb_at("U2", [128, 2, 2048], BF16, o, g=1); o += 8192
        SQb = sb_at("SQb", [128, 2, 512], BF16, o, g=1); o += 2048
        QG = sb_at("QG", [128, 2, 512], BF16, o, g=1); o += 2048
        LR = sb_at("LR", [128, 4, 512], F32, o, g=1); o += 8192
        T1 = sb_at("T1", [128, 2, 512], F32, o, g=1); o += 4096
        T2 = sb_at("T2", [128, 2, 512], F32, o, g=1); o += 4096
        HID = sb_at("HIDc", [128, 2, 2, 128], BF16, o, g=1); o += 1024
        GL = sb_at("GL", [128, 4, 128], F32, o, g=1); o += 2048
        assert o <= SB_END
        psA = PsRing([0, 1, 2, 3])
        psB = PsRing([4, 5])
        psC = PsRing([6, 7])

        if "no_prmb" not in dbg:
            P.copy(PRMB[:, :], prm(l, P_PE, 32))
        if "no_vcx" not in dbg:
            for g in range(2):
                P.memset(V(VCX.t[:, g, 64:65], VCX[:, :, :].regs), 1.0)
                P.copy(V(VCX.t[:, g, 65:97], VCX[:, :, :].regs), cb(K_OV, 32))
        if "no_ones" not in dbg:
            P.memset(V(KWZ.t[64:128, :, :, :], KWZ.allregs()), 0.0)
            P.memset(V(VS.t[:, :, :, 64:65], VS.allregs()), 1.0)
            P.memset(V(VW.t[:, :, :, 64:65], VW.allregs()), 1.0)

        if "no_wload" not in dbg:
            WV = loadw(w_in[l, :, C_VS:C_VS + 408], 8, 408)
            WQ = loadw(w_in[l, :, C_Q:C_Q + 512], 8, 512)
            WK = loadw(w_in[l, :, C_KC:C_KC + 512], 8, 512)
            W1s = [loadw(cmp_w1[l * 2 + kv, :, :], 16, 256) for kv in range(2)]
            for kv in range(2):
                P.dma(W2C[:, kv, :, :], NR(cmp_w2[l * 2 + kv, :, :].rearrange("(k p) n -> p k n", p=128)), q="pool")
        for i in range(NT if "no_tm" not in dbg else 0):
            tg, off = i // 4, (i % 4) * 128
            ps = psA.next()
            for kc in range(8):
                P.mm(ps[:, 0:408], H[:, kc, tg, off:off + 128], WV(kc, 0, 408), start=(kc == 0), stop=(kc == 7))
            if "no_tmc" in dbg:
                continue
            if "no_tma" not in dbg:
                P.copy(V(VS.t[:, i, :, 0:64], VS.regs_for((0, i))), V(ps.t[:, 0:128].rearrange("p (g d) -> p g d", g=2), ps[:, :].regs), eng="act")
            if "no_tmb" not in dbg:
                P.copy(V(VW.t[:, i, :, 0:64], VW.regs_for((0, i))), V(ps.t[:, 256:384].rearrange("p (g d) -> p g d", g=2), ps[:, :].regs))
            if "no_tms" not in dbg:
                P.act(GT[:, i, :], ps[:, 384:408], AF.Sigmoid)

        if upto == "nsa1a":
            P.barrier()
            return None
        rope_n = [0]

        def rope_a(ps, gcol):
            k = rope_n[0] % 2; rope_n[0] += 1
            P.act(SQb[:, k, :], ps[:, :], AF.Square)
            P.act(QG[:, k, :], ps[:, :], AF.Identity, scale=prm(l, gcol))
            return k

        def rope_b(k, tg, outs):
            pm = psB.next()
            P.mm(pm[:, :], cb(K_B64), SQb[:, k, :])
            pr = psC.next()
            P.mm(pr[:, :], cb(K_ROT), QG[:, k, :])
            P.act(LR[:, 2 * k, :], pm[:, :], AF.Ln, bias=EPS)
            P.act(LR[:, 2 * k + 1, :], LR[:, 2 * k, :], AF.Exp, scale=-0.5)
            P.tt(T1[:, k, :], QG[:, k, :], COS[:, tg, :], ALU.mult, eng="pool")
            P.tt(T2[:, k, :], pr[:, :], SIN[:, tg, :], ALU.mult)
            P.tt(T1[:, k, :], T1[:, k, :], T2[:, k, :], ALU.add)
            for (ov, p0, p1) in outs:
                P.tt(ov, V(T1.t[p0:p1, k, :], T1.regs_for((0, k))), V(LR.t[p0:p1, 2 * k + 1, :], LR.regs_for((0, 2 * k + 1))), ALU.mult,
                     eng="dve")

        jobs = []
        for m in range(4):
            for tg in range(4):
                jobs.append((WQ, m * 128, P_QG, tg, [(QT[:, m, tg, :], 0, 128)]))
        for tg in range(4):
            jobs.append((WK, 0, P_KG + 0, tg, [(V(KCV.t[:, 0, tg * 512:(tg + 1) * 512], KCV.regs_for((0, 0))), 0, 128)]))
            jobs.append((WK, 128, None, tg, [(V(KCV.t[:, 1, tg * 512:(tg + 1) * 512], KCV.regs_for((0, 1))), 0, 128)]))
            jobs.append((WK, 256, P_KG + 1, tg, [(V(KE.t[0:64, 0, tg, :], KE.regs_for((0, 0, tg))), 0, 64),
                                                 (V(KE.t[0:64, 1, tg, :], KE.regs_for((0, 1, tg))), 64, 128)]))
            jobs.append((WV, 128, P_KG + 2, tg, [(V(KWZ.t[0:64, 0, tg, :], KWZ.regs_for((0, 0, tg))), 0, 64),
                                                (V(KWZ.t[0:64, 1, tg, :], KWZ.regs_for((0, 1, tg))), 64, 128)]))
        pend = None
        for (wacc, c0, gcol, tg, outs) in jobs:
            ps = psA.next()
            proj_fm(ps, wacc, c0, 8, H, tg)
            if gcol is None:
                P.copy(outs[0][0], ps[:, :], eng="act")
                continue
            k = rope_a(ps, gcol)
            if pend is not None:
                rope_b(*pend)
            pend = (k, tg, outs)
        rope_b(*pend)

        if upto == "nsa1b":
            P.barrier()
            return None
        for kv in range(2):
            W1 = W1s[kv]

            def W2(nch, c0, c1, kv=kv):
                return V(W2C.t[:, kv, nch, c0:c1], W2C.regs_for((0, kv)))
            for nch in range(2):
                ps = psB.next()
                for l2 in range(16):
                    P.mm(ps[:, 0:1], W1(l2, nch * 128, nch * 128 + 128), PRMB[:, kv * 16 + l2: kv * 16 + l2 + 1],
                         start=(l2 == 0), stop=(l2 == 15))
                P.tt(CBIAS[:, kv * 2 + nch: kv * 2 + nch + 1], ps[:, 0:1], prm(l, P_B1 + kv * 2 + nch), ALU.add)
            for g in range(2):
                ub = (kv * 2 + g) % 2
                u2r = U2.regs_for((0, ub))
                P.copy(V(U2.t[0:64, ub, :], u2r), V(KCV.t[64 * g:64 * g + 64, kv, :], KCV.regs_for((0, kv))))
                P.copy(V(U2.t[64:128, ub, 0:2047], u2r), V(KCV.t[64 * g:64 * g + 64, kv, 1:2048], KCV.regs_for((0, kv))), eng="act")
                P.memset(V(U2.t[64:128, ub, 2047:2048], u2r), 0.0)
                for nch in range(2):
                    ps = psA.next()
                    for l2 in range(16):
                        rhs = V(bass.AP(U2.t, ub * 2048 + 2 * l2, [[2 * 2048, 128], [16, 127]]), u2r)
                        P.mm(ps[:, 0:127], W1(l2, nch * 128, nch * 128 + 128), rhs, start=(l2 == 0), stop=(l2 == 15))
                    xh = GL[:, 0, 0:127]; x2 = GL[:, 1, 0:127]; x3 = GL[:, 2, 0:127]; sg = GL[:, 3, 0:127]
                    P.act(xh, ps[:, 0:127], AF.Identity, bias=CBIAS[:, kv * 2 + nch: kv * 2 + nch + 1])
                    P.tt(x2, xh, xh, ALU.mult)
                    P.tt(x3, x2, xh, ALU.mult)
                    P.stt(x2, x3, 0.044715, xh, ALU.mult, ALU.add)
                    P.act(sg, x2, AF.Sigmoid, scale=1.5957691216057308)
                    P.tt(HID[:, g, nch, 0:127], xh, sg, ALU.mult)
                ps2 = psC.next()
                if kv == 0:
                    for nch in range(2):
                        P.mm(ps2[0:64, 0:127], W2(nch, 0, 64), HID[:, g, nch, 0:127], start=(nch == 0), stop=(nch == 1))
                    P.copy(V(KC.t[64 * g:64 * g + 64, 0:127], KC[:, :].regs), ps2[0:64, 0:127])
                else:
                    for nch in range(2):
                        P.mm(ps2[0:127, 0:64], HID[:, g, nch, 0:127], W2(nch, 0, 64), start=(nch == 0), stop=(nch == 1))
                    P.copy(V(VCX.t[0:127, g, 0:64], VCX[:, :, :].regs), ps2[0:127, 0:64])

        if "q" in taps:
            tq = nc.dram_tensor("tap_q", [128, 4 * 2048], BF16, kind="ExternalOutput"); tap_out["q"] = tq
            P.dma(NR(tq[:, :]), V(QT.t[:, :, :, :].rearrange("p a b c -> p (a b c)"), QT.allregs()))
            tk = nc.dram_tensor("tap_ke", [128, 2 * 2048], BF16, kind="ExternalOutput"); tap_out["ke"] = tk
            P.dma(NR(tk[0:96, :]), V(KE.t[0:96, :, :, :].rearrange("p a b c -> p (a b c)"), KE.allregs()))
            tk = nc.dram_tensor("tap_kc", [128, 128], BF16, kind="ExternalOutput"); tap_out["kc"] = tk
            P.dma(NR(tk[:, 0:127]), V(KC.t[:, 0:127], KC[:, :].regs))
            tk = nc.dram_tensor("tap_vcx", [128, 2 * 97], BF16, kind="ExternalOutput"); tap_out["vcx"] = tk
            P.dma(NR(tk[0:127, :]), V(VCX.t[0:127, :, :].rearrange("p a b -> p (a b)"), VCX[:, :, :].regs))
            tk = nc.dram_tensor("tap_vs", [128, 16 * 130], BF16, kind="ExternalOutput"); tap_out["vs"] = tk
            P.dma(NR(tk[:, :]), V(VS.t[:, :, :, :].rearrange("p a b c -> p (a b c)"), VS.allregs()))
            tk = nc.dram_tensor("tap_gt", [128, 16 * 24], F32, kind="ExternalOutput"); tap_out["gt"] = tk
            P.dma(NR(tk[:, :]), V(GT.t[:, :, :].rearrange("p a b -> p (a b)"), GT.allregs()))

        P.barrier()
        if upto == "nsa1":
            return None
        o = o1
        ET = sb_at("ET", [128, 24, 512], BF16, o, g=1); o += 24576
        QSEL = sb_at("QSEL", [128, 4, 512], BF16, o, g=1); o += 4096
        OTL = sb_at("OTL", [128, 2, 512], F32, o, g=1); o += 4096
        OB = sb_at("OB", [128, 2, 256], F32, o, g=1); o += 2048
        SM = sb_at("SM", [128, 8, 8], F32, o, g=1); o += 256
        IM = sb_at("IM", [128, 4, 128], F32, o, g=1); o += 2048
        SC = sb_at("SC", [128, 4, 96], F32, o, g=1); o += 1536
        M8 = sb_at("M8", [128, 4, 16], F32, o, g=1); o += 256
        assert o <= SB_END
        P.memset(V(QSEL.t[64:128, :, :], QSEL.allregs()), 0.0)
        psS = PsRing([0, 1, 2])
        psAcc = PsRing([3, 4, 5])
        psM = PsRing([6, 7])
        eti = [0]
        smi = [0]

        def q4(i, g):
            tg, off = i // 4, (i % 4) * 128
            return V(QT.t[64 * g:64 * g + 64, :, tg, off:off + 128], QT.regs_for((0, slice(None), tg)))

        def norm_acc(acc, i, g, branch, ncol):
            k = smi[0] % 8; smi[0] += 1
            a4 = acc.t[:, 0:4 * ncol].rearrange("p (r c) -> p r c", r=4)
            den = V(a4[:, :, 64], acc[:, :].regs)
            rd = V(SM.t[:, k, 0:4], SM.regs_for((0, k)))
            cf = V(SM.t[:, k, 4:8], SM.regs_for((0, k)))
            P.ts(rd, den, 1e-30, ALU.max)
            P.add("dve", lambda e: e.reciprocal(rd.ap, rd.ap), reads=[rd], writes=[rd])
            gv = V(bass.AP(GT.t, i * 24 + g * 12 + branch, [[16 * 24, 128], [3, 4]]), GT.regs_for((0, i)))
            P.tt(cf, rd, gv, ALU.mult)
            cfb = V(bass.AP(SM.t, k * 8 + 4, [[64, 128], [1, 4], [0, 64]]), SM.regs_for((0, k)))
            num = V(a4[:, :, 0:64], acc[:, :].regs)
            ok = i % 2
            dst = V(OTL.t[:, ok, g * 256:(g + 1) * 256].rearrange("p (r d) -> p r d", r=4), OTL.regs_for((0, ok)))
            if branch == 0:
                P.tt(dst, num, cfb, ALU.mult)
            else:
                kb = (i * 2 + g) % 2
                ob = V(OB.t[:, kb, :].rearrange("p (r d) -> p r d", r=4), OB.regs_for((0, kb)))
                P.tt(ob, num, cfb, ALU.mult)
                P.tt(dst, dst, ob, ALU.add)
            return rd, k

        def S_cmp(i, g):
            sc = psS.next()
            P.mm(sc[0:127, :], V(KC.t[64 * g:64 * g + 64, 0:127], KC[:, :].regs), q4(i, g))
            ek = eti[0] % 24; eti[0] += 1
            e = ET[0:127, ek, :]
            P.act(e, sc[0:127, :], AF.Exp, scale=0.125)
            e3 = V(ET.t[0:127, ek, :].rearrange("p (r q) -> p r q", r=4), ET.regs_for((0, ek)))
            msk = V(bass.AP(CM.t, i * 128, [[2048, 127], [0, 4], [1, 128]]), CM[:, :].regs)
            P.tt(e3, e3, msk, ALU.mult)
            qs = (i % 2) * 2 + g
            qsr = QSEL.regs_for((0, qs))
            P.copy(V(QSEL.t[0:64, qs, :].rearrange("p (r q) -> p r q", r=4), qsr), q4(i, g), eng="dve")
            return dict(kind="cmp", i=i, g=g, ek=ek, qs=qs)

        def V_cmp(c):
            i, g, ek, qs = c["i"], c["g"], c["ek"], c["qs"]
            acc = psAcc.next()
            for r in range(4):
                P.mm(acc[:, r * 97:(r + 1) * 97], V(ET.t[0:127, ek, r * 128:(r + 1) * 128], ET.regs_for((0, ek))),
                     V(VCX.t[0:127, g, :], VCX[:, :, :].regs))
            rd, k = norm_acc(acc, i, g, 0, 97)
            if i >= 8:
                a4 = acc.t[:, 0:388].rearrange("p (r c) -> p r c", r=4)
                kk = qs
                im = V(IM.t[:, kk, :].rearrange("p (r j) -> p r j", r=4), IM.regs_for((0, kk)))
                rdb = V(bass.AP(SM.t, k * 8, [[64, 128], [1, 4], [0, 32]]), SM.regs_for((0, k)))
                P.tt(im, V(a4[:, :, 65:97], acc[:, :].regs), rdb, ALU.mult)
                imr = V(bass.AP(IM.t, kk * 128, [[512, 128], [1, 32], [32, 4]]), IM.regs_for((0, kk)))
                scr = SC.regs_for((0, kk))
                score = V(SC.t[:, kk, 0:32], scr); s2 = V(SC.t[:, kk, 32:64], scr); sn = V(SC.t[:, kk, 64:96], scr)
                P.add("dve", lambda e_: e_.tensor_reduce(score.ap, imr.ap, AX.X, ALU.add), reads=[imr], writes=[score])
                P.tt(score, score, V(CF.t[:, 128 + (i - 8) * 32: 128 + (i - 7) * 32], CF[:, :].regs), ALU.add)
                m8r = M8.regs_for((0, kk))
                ma = V(M8.t[:, kk, 0:8], m8r); mb = V(M8.t[:, kk, 8:16], m8r)
                P.add("dve", lambda e_: e_.max(ma.ap, score.ap), reads=[score], writes=[ma])
                P.add("dve", lambda e_: e_.match_replace(s2.ap, ma.ap, score.ap, -1e9), reads=[score, ma], writes=[s2])
                P.add("dve", lambda e_: e_.max(mb.ap, s2.ap), reads=[s2], writes=[mb])
                P.ts(sn, score, V(M8.t[:, kk, 15:16], m8r), ALU.is_lt, -BIGSEL, ALU.mult)
                c["sn"] = sn

        def T_cmp(c):
            if c["i"] < 8:
                return
            qs = c["qs"]
            qsr = QSEL.regs_for((0, qs))
            pt = psM.next()
            P.transpose(pt[0:32, 0:128], c["sn"], IDF)
            src = V(bass.AP(pt.t, 0, [[512, 32], [0, 4], [1, 128]]), pt[:, :].regs)
            dst = V(bass.AP(QSEL.t, 64 * 2048 + qs * 512, [[2048, 32], [128, 4], [1, 128]]), qsr)
            P.copy(dst, src)

        def S_keys(i, g, branch):
            if branch == 1:
                js = list(range(0, i + 1))
            else:
                js = list(range(max(0, i - 4), i + 1))
            qs = (i % 2) * 2 + g
            es = []
            for j in js:
                sc = psS.next()
                jt, joff = j // 4, (j % 4) * 128
                if branch == 1:
                    P.mm(sc[:, :], V(KE.t[:, g, jt, joff:joff + 128], KE.regs_for((0, g, jt))),
                         V(QSEL.t[:, qs, :], QSEL.regs_for((0, qs))))
                else:
                    P.mm(sc[:, :], V(KWZ.t[:, g, jt, joff:joff + 128], KWZ.regs_for((0, g, jt))),
                         V(QSEL.t[:, qs, :], QSEL.regs_for((0, qs))))
                ek = eti[0] % 24; eti[0] += 1
                P.act(ET[:, ek, :], sc[:, :], AF.Exp, scale=0.125)
                if j == i:
                    P.tt(ET[:, ek, :], ET[:, ek, :], cb(K_DIAG, 512), ALU.mult)
                elif branch == 2 and j == i - 4:
                    P.tt(ET[:, ek, :], ET[:, ek, :], cb(K_LOW, 512), ALU.mult)
                es.append(ek)
            return dict(kind="keys", i=i, g=g, branch=branch, js=js, es=es)

        def V_keys(c):
            i, g, branch, js, es = c["i"], c["g"], c["branch"], c["js"], c["es"]
            acc = psAcc.next()
            VV = VS if branch == 1 else VW
            for r in range(4):
                for n, j in enumerate(js):
                    P.mm(acc[:, r * 65:(r + 1) * 65], V(ET.t[:, es[n], r * 128:(r + 1) * 128], ET.regs_for((0, es[n]))),
                         V(VV.t[:, j, g, :], VV.regs_for((0, j))), start=(n == 0), stop=(n == len(js) - 1))
            norm_acc(acc, i, g, branch, 65)

        def finish_tile(i):
            ok = i % 2
            tg, off = i // 4, (i % 4) * 128
            pt = psM.next()
            for c in range(4):
                P.transpose(pt[:, c * 128:(c + 1) * 128], OTL[:, ok, c * 128:(c + 1) * 128], IDF)
            P.copy(V(OT.t[:, :, tg, off:off + 128], OT.regs_for((0, slice(None), tg))),
                   V(pt.t[:, :].rearrange("p (c q) -> p c q", c=4), pt[:, :].regs), eng="dve")

        units = [("cmp", 0, 0), ("cmp", 0, 1)]
        for i in range(NT):
            for g in range(2):
                units.append(("win", i, g))
                if i + 1 < NT:
                    units.append(("cmp", i + 1, g))
                units.append(("slc", i, g))
            units.append(("fin", i, 0))

        def S_unit(u):
            kind, i, g = u
            if kind == "cmp":
                return S_cmp(i, g)
            if kind == "win":
                return S_keys(i, g, 2)
            if kind == "slc":
                return S_keys(i, g, 1)
            return dict(kind="fin", i=i)

        def V_unit(c):
            if c["kind"] == "cmp":
                V_cmp(c)
            elif c["kind"] == "keys":
                V_keys(c)

        ctxs = [None] * len(units)
        ctxs[0] = S_unit(units[0])
        deferred = []
        for k in range(len(units)):
            if k + 1 < len(units):
                if units[k + 1][0] == "slc":
                    for d in [d for d in deferred if d[2] == ("T", units[k + 1][1], units[k + 1][2])]:
                        d[1](); deferred.remove(d)
                ctxs[k + 1] = S_unit(units[k + 1])
            c = ctxs[k]
            V_unit(c)
            for d in [d for d in deferred if d[0] <= k]:
                d[1](); deferred.remove(d)
            if c["kind"] == "cmp" and c["i"] >= 8:
                deferred.append((k + 2, (lambda cc=c: T_cmp(cc)), ("T", c["i"], c["g"])))
            if c["kind"] == "fin":
                deferred.append((k + 1, (lambda ii=c["i"]: finish_tile(ii)), ("F", c["i"], 0)))
        for d in deferred:
            d[1]()

        if "ot" in taps:
            tk = nc.dram_tensor("tap_ot", [128, 4 * 2048], BF16, kind="ExternalOutput"); tap_out["ot"] = tk
            P.dma(NR(tk[:, :]), V(OT.t[:, :, :, :].rearrange("p a b c -> p (a b c)"), OT.allregs()))
        P.barrier()
        if upto == "nsa2":
            return None
        o = O_MIXED + 32768
        SG = Ring(sb_at("SG", [128, 2, 512], F32, o, g=1), 2); o += 4096
        MM = Ring(sb_at("MMt", [128, 2, 512], F32, o, g=1), 2); o += 4096
        out_branch(l, OT, lambda kc, tg: OT[:, kc, tg, :], nsa_w_out, C_GC, True, MIXED, SG, MM, PsRing([0, 1, 2, 3, 4, 5]))
        P.barrier()
        return MIXED

    def stage_a(l, MIXED):
        o = O_PH
        CH = sb_at("CH", [128, 2 + 2048], F32, o); o += 8256
        VA = sb_at("VA", [128, 4, 4, 512], BF16, o, g=2); o += 16384
        CSB = sb_at("CSB", [128, 2, 512], F32, o, g=1); o += 4096
        VT = sb_at("VT", [128, 2, 512], F32, o, g=1); o += 4096
        assert o <= O_MIXED
        o = O_MIXED + 32768
        SG = Ring(sb_at("SG", [128, 2, 512], F32, o, g=1), 2); o += 4096
        MM = Ring(sb_at("MMt", [128, 2, 512], F32, o, g=1), 2); o += 4096
        psr = PsRing([0, 1, 2, 3, 4, 5])
        WC = loadw(w_in[l, :, C_AC:C_AC + 512], 8, 512)
        WH = loadw(w_in[l, :, C_AH:C_AH + 512], 8, 512)
        WB = loadw(w_in[l, :, C_AB:C_AB + 512], 8, 512)
        chr_ = CH[:, :].regs
        n = 0
        for c in range(4):
            P.memset(V(CH.t[:, 0:2], chr_), 0.0)
            for tg in range(4):
                pc = psr.next(); proj_fm(pc, WC, c * 128, 8, H, tg)
                ph = psr.next(); proj_fm(ph, WH, c * 128, 8, H, tg)
                pb = psr.next(); proj_fm(pb, WB, c * 128, 8, H, tg)
                k = n % 2; n += 1
                P.copy(CSB[:, k, :], pc[:, :], eng="act")
                t0 = 2 + tg * 512
                P.tt(V(CH.t[:, t0:t0 + 512], chr_), CSB[:, k, :], ph[:, :], ALU.mult)
                wc = P_ACW + c * 3
                P.ts(VT[:, k, :], V(CH.t[:, t0:t0 + 512], chr_), prm(l, wc + 2), ALU.mult)
                P.stt(VT[:, k, :], V(CH.t[:, t0 - 1:t0 + 511], chr_), prm(l, wc + 1), VT[:, k, :], ALU.mult, ALU.add)
                P.stt(VT[:, k, :], V(CH.t[:, t0 - 2:t0 + 510], chr_), prm(l, wc + 0), VT[:, k, :], ALU.mult, ALU.add)
                P.tt(VA[:, c, tg, :], VT[:, k, :], pb[:, :], ALU.mult)
        if "va" in taps:
            tk = nc.dram_tensor("tap_va", [128, 4 * 2048], BF16, kind="ExternalOutput"); tap_out["va"] = tk
            P.dma(NR(tk[:, :]), V(VA.t[:, :, :, :].rearrange("p a b c -> p (a b c)"), VA.allregs()))
        out_branch(l, VA, lambda kc, tg: VA[:, kc, tg, :], a_w_out, C_GA, False, MIXED, SG, MM, psr)
        P.barrier()

    def stage_b(l, MIXED):
        o = O_PH
        U = sb_at("U", [128, 4, 2080], BF16, o, g=1); o += 16640
        UR = {(c, b): Reg(f"U{c}_{b}") for c in range(4) for b in range(5)}

        def ureg(c, a, b):
            return [UR[(c, k)] for k in range(a // 512, (b - 1) // 512 + 1)]
        DG = sb_at("DG", [128, 4, 31, 128], BF16, o, g=2); o += 31744
        assert o <= O_MIXED
        o = O_MIXED + 32768
        SG = Ring(sb_at("SG", [128, 2, 512], F32, o, g=1), 2); o += 4096
        MM = Ring(sb_at("MMt", [128, 2, 512], F32, o, g=1), 2); o += 4096
        SQ = sb_at("SQ", [128, 2, 512], BF16, o, g=1); o += 2048
        LR = sb_at("LRb", [128, 4, 512], F32, o, g=1); o += 8192
        assert o <= SB_END, o
        psr = PsRing([0, 1, 2, 3])
        psS = PsRing([4, 5, 6, 7])
        WA_ = loadw(w_in[l, :, C_BA:C_BA + 512], 8, 512)
        WG_ = loadw(w_in[l, :, C_BG:C_BG + 512], 8, 512)
        for c in range(4):
            for k in range(31):
                dgv = DG[:, c, k, :]; idv = cb(K_ID); wv = prm(l, P_BCW + c * 31 + k)
                P.add("pool", (lambda e, a=dgv, b=idv, w=wv: e.tensor_scalar(a.ap, b.ap, w.ap, 0.0, ALU.mult, op1=ALU.add)),
                      reads=[idv, wv, V(None, [P.tok])], writes=[dgv])
        for c in range(4):
            P.memset(V(U.t[:, c, 0:30], ureg(c, 0, 30)), 0.0)
            for tg in range(4):
                pa = psr.next(); proj_fm(pa, WA_, c * 128, 8, H, tg)
                pg = psr.next(); proj_fm(pg, WG_, c * 128, 8, H, tg)
                k = SG.next()
                P.act(SG.tt[:, k, :], pg[:, :], AF.Sigmoid)
                P.tt(V(U.t[:, c, 30 + tg * 512: 30 + (tg + 1) * 512], ureg(c, 30 + tg * 512, 30 + (tg + 1) * 512)), pa[:, :], SG.tt[:, k, :], ALU.mult)
        n = 0
        pend = [None]

        def flush_stats():
            if pend[0] is not None:
                pm_, pq_, uc_, sq_, c_ = pend[0]
                P.mm(pm_[:, :], cb(K_O512), uc_, start=(c_ == 0), stop=(c_ == 3))
                P.mm(pq_[:, :], cb(K_O512), sq_, start=(c_ == 0), stop=(c_ == 3))
                pend[0] = None

        def ln_tail(tg, pmean, pmsq):
            P.act(LR[:, 0, :], pmean[:, :], AF.Square)
            P.tt(LR[:, 1, :], pmsq[:, :], LR[:, 0, :], ALU.subtract)
            P.act(LR[:, 2, :], LR[:, 1, :], AF.Ln, bias=EPS)
            P.act(LR[:, 3, :], LR[:, 2, :], AF.Exp, scale=-0.5)
            for c in range(4):
                uc = V(U.t[:, c, tg * 512:(tg + 1) * 512], ureg(c, tg * 512, (tg + 1) * 512))
                m = MM.next()
                P.tt(MM.tt[:, m, :], uc, pmean[:, :], ALU.subtract)
                P.tt(MM.tt[:, m, :], MM.tt[:, m, :], LR[:, 3, :], ALU.mult)
                P.act(uc, MM.tt[:, m, :], AF.Silu, bias=prm(l, P_BLB + c), scale=prm(l, P_BLG + c))
        tails = []
        for tg in range(4):
            pmean = psS.next()
            pmsq = psS.next()
            for c in range(4):
                pcv = psr.next()
                for k in range(31):
                    P.mm(pcv[:, :], DG[:, c, k, :],
                         V(U.t[:, c, tg * 512 + k: tg * 512 + k + 512], ureg(c, tg * 512 + k, tg * 512 + k + 512)), start=(k == 0), stop=(k == 30))
                flush_stats()
                if tails:
                    ln_tail(*tails.pop())
                uc = V(U.t[:, c, tg * 512:(tg + 1) * 512], ureg(c, tg * 512, (tg + 1) * 512))
                kk = n % 2; n += 1
                P.act(SQ[:, kk, :], pcv[:, :], AF.Square, bias=prm(l, P_BCB + c))
                P.act(uc, pcv[:, :], AF.Identity, bias=prm(l, P_BCB + c))
                pend[0] = (pmean, pmsq, uc, SQ[:, kk, :], c)
            tails.append((tg, pmean, pmsq))
        flush_stats()
        ln_tail(*tails.pop())
        if "ub" in taps:
            tk = nc.dram_tensor("tap_ub", [128, 4, 2080], BF16, kind="ExternalOutput"); tap_out["ub"] = tk
            P.dma(NR(tk[:, :, 0:2048]), V(U.t[:, :, 0:2048], list(UR.values())))
        out_branch(l, U, lambda kc, tg: V(U.t[:, kc, tg * 512:(tg + 1) * 512], ureg(kc, tg * 512, (tg + 1) * 512)), b_w_out, C_GB, False,
                   MIXED, SG, MM, PsRing([0, 1, 2, 3, 4, 5]))
        P.barrier()

    def stage_wo(l, MIXED):
        o = O_PH
        XT = Ring(sb_at("wXT", [128, 6, 512], F32, o, g=1), 6); o += 12288
        psr = PsRing([0, 1, 2, 3, 4, 5])
        if "mixed" in taps:
            tk = nc.dram_tensor("tap_mixed", [128, 8 * 2048], BF16, kind="ExternalOutput"); tap_out["mixed"] = tk
            P.dma(NR(tk[:, :]), V(MIXED.t[:, :, :, :].rearrange("p a b c -> p (a b c)"), MIXED.allregs()))
        its = [(jg, jj, tg) for jg in range(2) for jj in range(4) for tg in range(4)]
        PF = 3
        slots = {}
        psr = PsRing([0, 1, 2, 3])
        SQw = Ring(sb_at("wSQ", [128, 3, 512], BF16, o, g=1), 3); o += 3072
        pend_stats = []

        def xload(n):
            jg, jj, tg = its[n]
            k = XT.next()
            slots[n] = k
            P.dma(XT.tt[:, k, :], Xsrc(jg * 4 + jj, tg), q="sp")
        for n in range(min(PF, len(its))):
            xload(n)
        WO = None
        for n, (jg, jj, tg) in enumerate(its):
            if jj == 0 and tg == 0:
                WO = loadw(w_o[l, :, jg * 512:(jg + 1) * 512], 8, 512)
            j = jg * 4 + jj
            ps = psr.next()
            proj_fm(ps, WO, jj * 128, 8, MIXED, tg)
            if n + PF < len(its):
                xload(n + PF)
            k = slots[n]
            P.tt(XT.tt[:, k, :], XT.tt[:, k, :], ps[:, :], ALU.add)
            P.dma(Xdst(j, tg), XT.tt[:, k, :], q="sp")
            q_ = SQw.next()
            P.act(SQw.tt[:, q_, :], XT.tt[:, k, :], AF.Square)
            while len(pend_stats) >= 2:
                pend_stats.pop(0)()
            pend_stats.append(lambda q_=q_, j=j, tg=tg: P.mm(PSB[4 + tg][:, :], cb(K_O1024), SQw.tt[:, q_, :],
                                                             start=(j == 0), stop=(j == 7)))
        for f in pend_stats:
            f()
        xstate["in_y"] = True

    def stage_ffn(l, want_stats):
        o = O_PH
        HID = sb_at("HID", [128, 22, 4, 512], BF16, o, g=2); o += 90112
        SU = Ring(sb_at("SU", [128, 2, 512], F32, o, g=1), 2)
        XT = Ring(sb_at("fXT", [128, 4, 512], F32, o, g=1), 4); o += 10240
        assert o <= SB_END
        psr = PsRing([0, 1, 2, 3, 4, 5])
        for hg in range(6):
            nc_ = min(512, 2816 - hg * 512)
            W1 = loadw(ffn_w13[l, :, hg * 512: hg * 512 + nc_], 8, nc_)
            W3 = loadw(ffn_w13[l, :, 2816 + hg * 512: 2816 + hg * 512 + nc_], 8, nc_)
            for jj in range(nc_ // 128):
                hc = hg * 4 + jj
                for tg in range(4):
                    pu = psr.next(); proj_fm(pu, W1, jj * 128, 8, H, tg)
                    pv = psr.next(); proj_fm(pv, W3, jj * 128, 8, H, tg)
                    k = SU.next()
                    P.act(SU.tt[:, k, :], pu[:, :], AF.Silu)
                    P.tt(HID[:, hc, tg, :], SU.tt[:, k, :], pv[:, :], ALU.mult)
        P.barrier()
        its = [(jo, tg) for jo in range(8) for tg in range(4)]
        PF = 3
        slots = {}
        psr = PsRing([0, 1, 2, 3])
        SQf = Ring(sb_at("fSQ", [128, 2, 512], BF16, O_PH + 90112 + 8192, g=1), 2)
        pend_stats = []

        def xload(n):
            jo, tg = its[n]
            k = XT.next()
            slots[n] = k
            P.dma(XT.tt[:, k, :], Xsrc(jo, tg), q="sp")
        for n in range(PF):
            xload(n)
        W2 = None
        for n, (jo, tg) in enumerate(its):
            if tg == 0:
                W2 = loadw(ffn_w2[l, :, jo * 128:(jo + 1) * 128], 22, 128)
            ps = psr.next()
            for kc in range(22):
                P.mm(ps[:, :], W2(kc, 0, 128), HID[:, kc, tg, :], start=(kc == 0), stop=(kc == 21))
            if n + PF < len(its):
                xload(n + PF)
            k = slots[n]
            P.tt(XT.tt[:, k, :], XT.tt[:, k, :], ps[:, :], ALU.add)
            P.dma(Xdst(jo, tg), XT.tt[:, k, :], q="sp")
            if want_stats:
                q_ = SQf.next()
                P.act(SQf.tt[:, q_, :], XT.tt[:, k, :], AF.Square)
                while len(pend_stats) >= 1:
                    pend_stats.pop(0)()
                pend_stats.append(lambda q_=q_, jo=jo, tg=tg: P.mm(PSB[4 + tg][:, :], cb(K_O1024), SQf.tt[:, q_, :],
                                                                   start=(jo == 0), stop=(jo == 7)))
        for f in pend_stats:
            f()
        P.barrier()

    P.barrier()
    for l in range(n_layers):
        if l == 0:
            stage_norm(l, P_N1)
        else:
            stage_norm_apply(l, P_N1)
        if "h" in taps and l == 0:
            tk = nc.dram_tensor("tap_h", [128, 8 * 2048], BF16, kind="ExternalOutput"); tap_out["h"] = tk
            P.dma(NR(tk[:, :]), V(H.t[:, :, :, :].rearrange("p a b c -> p (a b c)"), H.allregs()))
        P.barrier()
        if upto == "norm":
            break
        MIXED = stage_nsa(l)
        if upto in ("nsa1", "nsa2", "nsa3", "nsa1a", "nsa1b"):
            break
        stage_a(l, MIXED)
        if upto == "a":
            break
        stage_b(l, MIXED)
        if upto == "b":
            break
        stage_wo(l, MIXED)
        if upto == "wo":
            break
        stage_norm_apply(l, P_N2)
        P.barrier()
        stage_ffn(l, l + 1 < n_layers)
    P.emit()
    return nc, P, tap_out


_CACHE = {}


def kernel(**inputs):
    shared, xs = _prep(inputs)
    if "nc" not in _CACHE:
        _CACHE["nc"] = build()[0]
    nc = _CACHE["nc"]
    in_maps = []
    for b in range(8):
        m = dict(shared)
        m["xT"] = xs[b]
        in_maps.append(m)
    res = run_bass_kernel_spmd(nc, in_maps, core_ids=list(range(8)))
    out = np.stack([np.ascontiguousarray(np.asarray(r["yT"]).T) for r in res.results], axis=0)
    return out.astype(np.float32)
```

```python
import contextlib
import itertools
import numpy as np
import concourse.bass as bass
import concourse.mybir as mybir
from concourse.bass_utils import run_bass_kernel_spmd

F32 = mybir.dt.float32
BF16 = mybir.dt.bfloat16
AF = mybir.ActivationFunctionType
ALU = mybir.AluOpType
AX = mybir.AxisListType

SAME_ENGINE_SYNC = False
DMA_SLOTS = 8

L = 4
S = 2048
D = 1024
NT = 16
EPS = 1e-6
INW = 6936
C_AC, C_AB, C_AH = 0, 512, 1024
C_BA, C_BG = 1536, 2048
C_Q = 2560
C_KC, C_VC, C_KS, C_VS, C_KW, C_VW = 3072, 3200, 3328, 3456, 3584, 3712
C_GT = 3840
C_GA, C_GB, C_GC = 3864, 4888, 5912
P_N1, P_N2, P_ACW, P_BCW, P_BCB, P_BLG, P_BLB, P_QG, P_KG, P_B1, P_PE = 0, 8, 16, 28, 152, 156, 160, 164, 165, 168, 172
NPC = 204
K_ID, K_ROT, K_B64, K_O1024, K_O512, K_ONE, K_DIAG, K_LOW, K_OV = 0, 128, 256, 384, 512, 640, 768, 1280, 1792
NCB = 1824
BIGSEL = 30000.0


class Reg:
    __slots__ = ("name", "lw", "rd", "excl")

    def __init__(self, name, excl=False):
        self.name = name
        self.lw = None
        self.rd = []
        self.excl = excl


class V:
    __slots__ = ("ap", "regs")

    def __init__(self, ap, regs):
        self.ap = ap
        self.regs = regs if isinstance(regs, (list, tuple)) else [regs]


class TT:
    def __init__(self, t, name, g=0, excl=False):
        self.t = t
        self.name = name
        self.g = g
        self.regs = {}
        self.excl = excl

    def reg(self, idx):
        r = self.regs.get(idx)
        if r is None:
            r = self.regs[idx] = Reg(f"{self.name}{list(idx)}", self.excl)
        return r

    def regs_for(self, key):
        gi = key[1:1 + self.g]
        shape = self.t.shape
        rngs = []
        for d in range(self.g):
            i = gi[d] if d < len(gi) else slice(None)
            if isinstance(i, int):
                rngs.append([i])
            else:
                rngs.append(list(range(*i.indices(shape[1 + d]))))
        return [self.reg(tuple(ix)) for ix in itertools.product(*rngs)]

    def __getitem__(self, key):
        if not isinstance(key, tuple):
            key = (key,)
        return V(self.t[key], self.regs_for(key))

    def allregs(self):
        return self.regs_for((slice(None),))


class Op:
    __slots__ = ("waits", "fn", "signal", "dma")

    def __init__(self, fn):
        self.waits = []
        self.fn = fn
        self.signal = False
        self.dma = None


class Plan:
    ENG = ["pe", "act", "dve", "pool", "sp"]

    def __init__(self, nc):
        self.nc = nc
        self.ops = {e: [] for e in self.ENG}
        self.known = {e: {} for e in self.ENG}
        self.ndma = {e: 0 for e in self.ENG}
        self.tok = Reg("phase_token")

    def _need(self, eng, ev, op, raw=False):
        if ev is None:
            return
        if ev[0] == 'c':
            _, src, idx = ev
            if src == eng and eng == "pe":
                return
            k = self.known[eng]
            if k.get(src, -1) >= idx:
                return
            k[src] = idx
            self.ops[src][idx].signal = True
            op.waits.append(ev)
        else:
            _, q, slot, val = ev
            k = self.known[eng]
            if k.get((q, slot), 0) >= val:
                return
            k[(q, slot)] = val
            op.waits.append(ev)

    def add(self, eng, fn, reads=(), writes=(), dma=False):
        op = Op(fn)
        idx = len(self.ops[eng])
        rregs = []
        wregs = []
        for v in reads:
            if v is not None:
                for r in v.regs:
                    (wregs if r.excl else rregs).append(r)
        for v in writes:
            if v is not None:
                wregs.extend(v.regs)
        cand = {}

        def consider(ev, raw):
            if ev is None:
                return
            if ev[0] == 'c':
                if ev[1] == eng and eng == "pe":
                    return
                key = ('c', ev[1])
                if key not in cand or cand[key][2] < ev[2]:
                    cand[key] = ev
            else:
                key = ('d', ev[1], ev[2])
                if key not in cand or cand[key][3] < ev[3]:
                    cand[key] = ev
        for r in rregs:
            consider(r.lw, True)
        for r in wregs:
            consider(r.lw, False)
            for ev in r.rd:
                consider(ev, False)
        for ev in cand.values():
            self._need(eng, ev, op, raw=True)
        if dma:
            j = self.ndma[eng]
            self.ndma[eng] = j + 1
            slot = j % DMA_SLOTS
            val = 16 * (j // DMA_SLOTS + 1)
            if j >= DMA_SLOTS:
                self._need(eng, ('d', eng, slot, val - 16), op)
            op.dma = (slot, val)
            ev = ('d', eng, slot, val)
        else:
            ev = ('c', eng, idx)
        for r in rregs:
            r.rd.append(ev)
        for r in wregs:
            r.lw = ev
            r.rd = []
        self.ops[eng].append(op)
        return op

    def barrier(self):
        last = {}
        BENG = [e for e in self.ENG if e != "pool"]
        for e in self.ENG:
            n = len(self.ops[e])
            if n:
                for i in range(n - 1, -1, -1):
                    if self.ops[e][i].dma is None:
                        last[e] = ('c', e, i)
                        break
        dl = []
        for q in BENG:
            n = self.ndma[q]
            for s in range(min(DMA_SLOTS, n)):
                nd = (n - s + DMA_SLOTS - 1) // DMA_SLOTS
                dl.append(('d', q, s, 16 * nd))
        for e in ("act", "dve", "sp"):
            op = Op(lambda eng: eng.nop())
            for src, ev in last.items():
                self._need(e, ev, op)
            for ev in dl:
                self._need(e, ev, op)
            if op.waits:
                self.ops[e].append(op)
                if e == "dve":
                    self.tok.lw = ('c', 'dve', len(self.ops[e]) - 1)
                    self.tok.rd = []

    def emit(self):
        nc = self.nc
        cnt = {}
        for e in self.ENG:
            c = 0
            for i, op in enumerate(self.ops[e]):
                if op.signal and op.dma is None:
                    c += 1
                    cnt[(e, i)] = c
        with contextlib.ExitStack() as st:
            csem = {e: st.enter_context(nc.semaphore(f"c_{e}")) for e in self.ENG}
            dsem = {}
            for e in self.ENG:
                for s in range(min(DMA_SLOTS, self.ndma[e])):
                    dsem[(e, s)] = st.enter_context(nc.semaphore(f"d_{e}{s}"))
            block = st.enter_context(nc.Block())

            def replay(e):
                def body(eng):
                    for i, op in enumerate(self.ops[e]):
                        for ev in op.waits:
                            if ev[0] == 'c':
                                eng.wait_ge(csem[ev[1]], cnt[(ev[1], ev[2])])
                            else:
                                eng.wait_ge(dsem[(ev[1], ev[2])], ev[3])
                        ins = op.fn(eng)
                        if op.dma is not None:
                            ins.then_inc(dsem[(e, op.dma[0])], 16)
                        elif op.signal:
                            ins.then_inc(csem[e], 1)
                    if e == "sp":
                        for q in self.ENG:
                            n = self.ndma[q]
                            for s in range(min(DMA_SLOTS, n)):
                                nd = (n - s + DMA_SLOTS - 1) // DMA_SLOTS
                                eng.wait_ge(dsem[(q, s)], 16 * nd)
                return body

            block.tensor(replay("pe"))
            block.scalar(replay("act"))
            block.vector(replay("dve"))
            block.gpsimd(replay("pool"))
            block.sync(replay("sp"))

    def stats(self):
        return {e: (len(self.ops[e]), sum(len(o.waits) for o in self.ops[e])) for e in self.ENG}

    def mm(self, out, lhsT, rhs, start=True, stop=True):
        return self.add("pe", lambda e: e.matmul(out.ap, lhsT.ap, rhs.ap, start=start, stop=stop),
                        reads=[lhsT, rhs], writes=[out])

    def transpose(self, out, in_, ident):
        return self.add("pe", lambda e: e.transpose(out.ap, in_.ap, ident.ap),
                        reads=[in_, ident], writes=[out])

    def act(self, out, in_, func, bias=None, scale=None, eng="act"):
        def fn(e):
            kw = {}
            if bias is not None:
                kw["bias"] = bias.ap if isinstance(bias, V) else bias
            if scale is not None:
                kw["scale"] = scale.ap if isinstance(scale, V) else scale
            return e.activation(out.ap, in_.ap, func, **kw)
        rd = [in_] + [x for x in (bias, scale) if isinstance(x, V)]
        return self.add(eng, fn, reads=rd, writes=[out])

    def tt(self, out, in0, in1, op, eng="dve"):
        return self.add(eng, lambda e: e.tensor_tensor(out.ap, in0.ap, in1.ap, op),
                        reads=[in0, in1], writes=[out])

    def ts(self, out, in0, s1, op0, s2=None, op1=None, eng="dve"):
        def fn(e):
            a1 = s1.ap if isinstance(s1, V) else s1
            a2 = s2.ap if isinstance(s2, V) else s2
            kw = {}
            if op1 is not None:
                kw["op1"] = op1
            return e.tensor_scalar(out.ap, in0.ap, a1, a2, op0, **kw)
        rd = [in0] + [x for x in (s1, s2) if isinstance(x, V)]
        return self.add(eng, fn, reads=rd, writes=[out])

    def stt(self, out, in0, scalar, in1, op0, op1, eng="dve"):
        def fn(e):
            a = scalar.ap if isinstance(scalar, V) else scalar
            return e.scalar_tensor_tensor(out.ap, in0.ap, a, in1.ap, op0, op1)
        rd = [in0, in1] + ([scalar] if isinstance(scalar, V) else [])
        return self.add(eng, fn, reads=rd, writes=[out])

    def copy(self, out, in_, eng="dve"):
        if eng == "act":
            return self.add("act", lambda e: e.copy(out.ap, in_.ap), reads=[in_], writes=[out])
        return self.add(eng, lambda e: e.tensor_copy(out.ap, in_.ap), reads=[in_], writes=[out])

    def memset(self, out, val, eng="dve"):
        return self.add(eng, lambda e: e.memset(out.ap, val), writes=[out])

    def dma(self, out, in_, q="sp", **kw):
        return self.add(q, lambda e: e.dma_start(out.ap, in_.ap, **kw), reads=[in_], writes=[out], dma=True)


class Ring:
    def __init__(self, tt, n):
        self.tt = tt
        self.n = n
        self.i = 0

    def next(self):
        k = self.i % self.n
        self.i += 1
        return k


def bc_ap(v, pattern):
    return V(bass.AP(v.ap.tensor, v.ap.offset, pattern), v.regs)


def _consts():
    cb = np.zeros((128, NCB), np.float32)
    cb[:, K_ID:K_ID + 128] = np.eye(128, dtype=np.float32)
    rot = np.zeros((128, 128), np.float32)
    for m in range(128):
        if m % 64 < 32:
            rot[m + 32, m] = -1.0
        else:
            rot[m - 32, m] = 1.0
    cb[:, K_ROT:K_ROT + 128] = rot
    b64 = np.zeros((128, 128), np.float32)
    b64[:64, :64] = 1.0 / 64
    b64[64:, 64:] = 1.0 / 64
    cb[:, K_B64:K_B64 + 128] = b64
    cb[:, K_O1024:K_O1024 + 128] = 1.0 / 1024
    cb[:, K_O512:K_O512 + 128] = 1.0 / 512
    cb[:, K_ONE:K_ONE + 128] = 1.0
    kk = np.arange(128)[:, None]
    qq = np.arange(128)[None, :]
    cb[:, K_DIAG:K_DIAG + 512] = np.tile((kk <= qq).astype(np.float32), (1, 4))
    cb[:, K_LOW:K_LOW + 512] = np.tile((kk > qq).astype(np.float32), (1, 4))
    c = np.arange(128)[:, None]
    j = np.arange(32)[None, :]
    ov = ((16 * c < 64 * j + 64) & (16 * c + 32 > 64 * j) & (c < 127)).astype(np.float32)
    cb[:, K_OV:K_OV + 32] = ov
    em = np.zeros((128, 2048), np.float32)
    key = np.arange(2048)[None, :]
    em[64:96] = (key // 64 == np.arange(32)[:, None]).astype(np.float32)
    t = np.arange(2048)[None, :]
    cm = ((16 * c + 31 <= t) & (c < 127)).astype(np.float32)
    pos = np.arange(2048, dtype=np.float32)
    inv = (1.0 / (np.float32(10000.0) ** (np.arange(0, 64, 2, dtype=np.float32) / np.float32(64)))).astype(np.float32)
    ang = pos[:, None] * inv[None, :]
    cos = np.cos(ang).astype(np.float32)
    sin = np.sin(ang).astype(np.float32)
    pidx = (np.arange(128) % 64) % 32
    cosT = np.ascontiguousarray(cos[:, pidx].T)
    sinT = np.ascontiguousarray(sin[:, pidx].T)
    bs = np.zeros((128, 8, 32), np.float32)
    for ii in range(8):
        i = ii + 8
        for p in range(128):
            cur = 2 * i + (1 if p >= 64 else 0)
            for jj in range(32):
                if jj == 0 or jj == cur or jj == cur - 1:
                    bs[p, ii, jj] = 1.0e4
                elif jj <= cur:
                    bs[p, ii, jj] = 0.0
                else:
                    bs[p, ii, jj] = -1.0e4
    cf = np.zeros((128, 128 + 256), np.float32)
    cf[:, 0:128] = np.eye(128, dtype=np.float32)
    cf[:, 128:384] = bs.reshape(128, 256)
    return dict(cst_b=cb, cst_e=em, cst_cm=cm, cst_cos=cosT, cst_sin=sinT, cst_f=cf)


def _params(inp):
    pr = np.zeros((L, 128, NPC), np.float32)
    p = np.arange(128)
    for l in range(L):
        pr[l, :, P_N1:P_N1 + 8] = inp["norm1_g"][l].reshape(8, 128).T
        pr[l, :, P_N2:P_N2 + 8] = inp["norm2_g"][l].reshape(8, 128).T
        acw = inp["a_conv_w"][l]
        for c in range(4):
            pr[l, :, P_ACW + c * 3:P_ACW + c * 3 + 3] = acw[:, c * 128:(c + 1) * 128].T
        bcw = inp["b_conv_w"][l]
        for c in range(4):
            pr[l, :, P_BCW + c * 31:P_BCW + c * 31 + 31] = bcw[:, c * 128:(c + 1) * 128].T
        pr[l, :, P_BCB:P_BCB + 4] = inp["b_conv_b"][l].reshape(4, 128).T
        pr[l, :, P_BLG:P_BLG + 4] = inp["b_ln_g"][l].reshape(4, 128).T
        pr[l, :, P_BLB:P_BLB + 4] = inp["b_ln_b"][l].reshape(4, 128).T
        pr[l, :, P_QG] = inp["q_norm_g"][l][p % 64]
        for b in range(3):
            pr[l, :, P_KG + b] = inp["k_norm_g"][l, b][p % 64]
        for kv in range(2):
            pr[l, :, P_B1 + kv * 2:P_B1 + kv * 2 + 2] = inp["cmp_b1"][l, kv].reshape(2, 128).T
            pe = inp["cmp_pe"][l, kv]
            pe2 = pe.reshape(16, 2, 64).transpose(1, 2, 0).reshape(128, 16)
            pr[l, :, P_PE + kv * 16:P_PE + kv * 16 + 16] = pe2
    return pr


def _prep(inputs):
    inp = {k: np.asarray(v, dtype=np.float32) for k, v in inputs.items()}
    w_in = inp["w_in"].copy()
    q = inp["w_in"][:, :, C_Q:C_Q + 512].reshape(L, 1024, 2, 4, 64)
    w_in[:, :, C_Q:C_Q + 512] = q.transpose(0, 1, 3, 2, 4).reshape(L, 1024, 512)
    shared = dict(
        w_in=np.ascontiguousarray(w_in),
        a_w_out=inp["a_w_out"], b_w_out=inp["b_w_out"], nsa_w_out=inp["nsa_w_out"],
        w_o=inp["w_o"], ffn_w13=inp["ffn_w13"], ffn_w2=inp["ffn_w2"],
        cmp_w1=np.ascontiguousarray(inp["cmp_w1"].reshape(L * 2, 2048, 256)),
        cmp_w2=np.ascontiguousarray(inp["cmp_w2"].reshape(L * 2, 256, 64)),
        params=_params(inp),
    )
    shared.update(_consts())
    xs = [np.ascontiguousarray(inp["x"][b].T) for b in range(8)]
    return shared, xs


SB0 = 16512
SB_END = 229344
O_H = SB0
O_WP = O_H + 32768
O_CB = O_WP + 40960
O_CF = O_CB + 3712
O_PRM = O_CF + 1536
O_PRMB = O_PRM + 3328
O_KE = O_PRMB + 128
O_CM = O_KE + 8192
O_COS = O_CM + 4096
O_SIN = O_COS + 8192
O_W2C = O_SIN + 8192
O_PH = O_W2C + 512
PH_SIZE = SB_END - O_PH
NWSLOT = 5


def build(n_layers=L, taps=(), upto=None, dbg=()):
    nc = bass.Bass("TRN2", target_bir_lowering=False)
    P = Plan(nc)
    taps = set(taps)
    tap_out = {}

    def din(name, shape):
        return nc.dram_tensor(name, shape, F32, kind="ExternalInput")

    xT = din("xT", [D, S])
    yT = nc.dram_tensor("yT", [D, S], F32, kind="ExternalOutput")
    w_in = din("w_in", [L, D, INW])
    a_w_out = din("a_w_out", [L, 512, D])
    b_w_out = din("b_w_out", [L, 512, D])
    nsa_w_out = din("nsa_w_out", [L, 512, D])
    w_o = din("w_o", [L, D, D])
    ffn_w13 = din("ffn_w13", [L, D, 5632])
    ffn_w2 = din("ffn_w2", [L, 2816, D])
    cmp_w1 = din("cmp_w1", [L * 2, 2048, 256])
    cmp_w2 = din("cmp_w2", [L * 2, 256, 64])
    params = din("params", [L, 128, NPC])
    cst_b = din("cst_b", [128, NCB])
    cst_e = din("cst_e", [128, 2048])
    cst_cm = din("cst_cm", [128, 2048])
    cst_cos = din("cst_cos", [128, 2048])
    cst_sin = din("cst_sin", [128, 2048])
    cst_f = din("cst_f", [128, 384])

    def NR(ap):
        return V(ap, [])

    names = itertools.count()

    def sb_at(name, shape, dt, off, g=0):
        size = int(np.prod(shape[1:])) * (2 if dt == BF16 else 4)
        assert off + size <= SB_END, (name, off, size)
        t = nc.alloc_sbuf_tensor_at(f"{name}_{next(names)}", list(shape), dt, offset=off)
        return TT(t, name, g)

    def tap(name, view, shape):
        if name in taps:
            o = nc.dram_tensor("tap_" + name, list(shape), F32, kind="ExternalOutput")
            tap_out[name] = o
            return o
        return None

    H = sb_at("H", [128, 8, 4, 512], BF16, O_H, g=2)
    WSL = [sb_at(f"W{i}", [128, 4096], BF16, O_WP + i * 8192) for i in range(NWSLOT)]
    CB = sb_at("CB", [128, NCB], BF16, O_CB)
    CF = sb_at("CF", [128, 384], F32, O_CF)
    PRM = sb_at("PRM", [128, L, NPC], F32, O_PRM)
    PRMB = sb_at("PRMB", [128, 32], BF16, O_PRMB)
    KE = sb_at("KE", [128, 2, 4, 512], BF16, O_KE, g=2)
    CM = sb_at("CM", [128, 2048], BF16, O_CM)
    COS = sb_at("COS", [128, 4, 512], F32, O_COS)
    SIN = sb_at("SIN", [128, 4, 512], F32, O_SIN)
    W2C = sb_at("W2C", [128, 2, 2, 64], BF16, O_W2C, g=1)
    PSB = [TT(nc.alloc_psum_tensor(f"psb{i}", [128, 512], F32), f"psb{i}", excl=True) for i in range(8)]

    class PsRing:
        def __init__(self, banks):
            self.b = banks
            self.i = 0

        def next(self):
            k = self.b[self.i % len(self.b)]
            self.i += 1
            return PSB[k]

    wslot_i = [0]

    def loadw(dram_ap_2d, nk, ncols):
        slot = WSL[wslot_i[0] % NWSLOT]
        wslot_i[0] += 1
        assert nk * ncols <= 4096
        dst = V(slot.t[:, 0:nk * ncols].rearrange("p (k n) -> p k n", k=nk), slot[:, :].regs)
        src = NR(dram_ap_2d.rearrange("(k p) n -> p k n", p=128))
        P.dma(dst, src, q="pool")

        def acc(kc, c0, c1):
            return V(slot.t[:, kc * ncols + c0: kc * ncols + c1], slot[:, :].regs)
        return acc

    def prm(l, col, n=1):
        return V(PRM.t[:, l, col:col + n], PRM[:, :, :].regs)

    def cb(col, n=128, p0=0, p1=128):
        return V(CB.t[p0:p1, col:col + n], CB[:, :].regs)

    P.dma(CB[:, :], NR(cst_b[:, :]), q="pool")
    P.dma(CF[:, :], NR(cst_f[:, :]), q="sp")
    P.dma(PRM[:, :, :], NR(params.rearrange("l p n -> p l n")), q="sp")
    P.dma(V(COS.t[:, :, :], COS.allregs()), NR(cst_cos.rearrange("p (a b) -> p a b", a=4)), q="sp")
    P.dma(V(SIN.t[:, :, :], SIN.allregs()), NR(cst_sin.rearrange("p (a b) -> p a b", a=4)), q="sp")
    P.dma(CM[:, :], NR(cst_cm[:, :]), q="pool")
    for g in range(2):
        P.dma(V(KE.t[64:96, g, :, :], KE.regs_for((0, g))), NR(cst_e[64:96, :].rearrange("p (a b) -> p a b", a=4)), q="pool")
    P.memset(V(KE.t[96:128, :, :, :], KE.allregs()), 0.0)
    IDF = V(CF.t[:, 0:128], CF[:, :].regs)

    Xreg = {(c, tg): Reg(f"X{c}_{tg}") for c in range(8) for tg in range(4)}
    xstate = {"in_y": False}

    def Xsrc(c, tg):
        if xstate["in_y"]:
            return V(yT[c * 128:(c + 1) * 128, tg * 512:(tg + 1) * 512], [Xreg[(c, tg)]])
        return NR(xT[c * 128:(c + 1) * 128, tg * 512:(tg + 1) * 512])

    def Xdst(c, tg):
        return V(yT[c * 128:(c + 1) * 128, tg * 512:(tg + 1) * 512], [Xreg[(c, tg)]])

    def dump(name, view_fn, shape, nparts=128):
        pass

    def stage_norm(l, gcol):
        o = O_PH
        XT = sb_at("nXT", [128, 16, 512], F32, o, g=1); o += 32768
        SQ = sb_at("nSQ", [128, 4, 512], BF16, o, g=1); o += 4096
        LR = sb_at("nLR", [128, 4, 512], F32, o, g=1); o += 8192
        psr = PsRing([6, 7])
        xi = 0
        for tg in range(4):
            ps = psr.next()
            xs = []
            for c in range(8):
                k = xi % 16; xi += 1
                P.dma(XT[:, k, :], Xsrc(c, tg), q="sp")
                sq = SQ[:, k % 4, :]
                P.act(sq, XT[:, k, :], AF.Square)
                P.mm(ps[:, :], cb(K_O1024), sq, start=(c == 0), stop=(c == 7))
                xs.append(k)
            lk = (tg % 2) * 2
            P.act(LR[:, lk, :], ps[:, :], AF.Ln, bias=EPS)
            P.act(LR[:, lk + 1, :], LR[:, lk, :], AF.Exp, scale=-0.5)
            for c in range(8):
                P.stt(H[:, c, tg, :], XT[:, xs[c], :], prm(l, gcol + c), LR[:, lk + 1, :], ALU.mult, ALU.mult)

    def stage_norm_apply(l, gcol):
        o = O_PH + 16384
        XA = Ring(sb_at("aXT", [128, 10, 512], F32, o, g=1), 10); o += 20480
        RS = sb_at("aRS", [128, 4, 512], F32, o, g=1); o += 8192
        LT = sb_at("aLT", [128, 2, 512], F32, o, g=1); o += 4096
        assert o <= O_MIXED
        its = [(tg, c) for tg in range(4) for c in range(8)]
        PF = 8
        slots = {}

        def xload(n):
            tg, c = its[n]
            k = XA.next()
            slots[n] = k
            P.dma(XA.tt[:, k, :], Xsrc(c, tg), q="sp")
        for n in range(PF):
            xload(n)
        for n, (tg, c) in enumerate(its):
            if c == 0:
                P.act(LT[:, tg % 2, :], PSB[4 + tg][:, :], AF.Ln, bias=EPS)
                P.act(RS[:, tg, :], LT[:, tg % 2, :], AF.Exp, scale=-0.5)
            k = slots[n]
            P.stt(H[:, c, tg, :], XA.tt[:, k, :], prm(l, gcol + c), RS[:, tg, :], ALU.mult, ALU.mult)
            if n + PF < len(its):
                xload(n + PF)

    def proj_fm(ps, wacc, c0, nk, src, tg):
        for kc in range(nk):
            P.mm(ps[:, :], wacc(kc, c0, c0 + 128), src[:, kc, tg, :], start=(kc == 0), stop=(kc == nk - 1))

    def out_branch(l, SRC, src_fn, wout, gcol, first, MIXED, SG, MM, psr):
        for jg in range(2):
            WO = loadw(wout[l, :, jg * 512:(jg + 1) * 512], 4, 512)
            WG = loadw(w_in[l, :, gcol + jg * 512: gcol + (jg + 1) * 512], 8, 512)
            for jj in range(4):
                j = jg * 4 + jj
                for tg in range(4):
                    py = psr.next()
                    for kc in range(4):
                        P.mm(py[:, :], WO(kc, jj * 128, jj * 128 + 128), src_fn(kc, tg), start=(kc == 0), stop=(kc == 3))
                    pg = psr.next()
                    proj_fm(pg, WG, jj * 128, 8, H, tg)
                    k = SG.next()
                    P.act(SG.tt[:, k, :], pg[:, :], AF.Sigmoid)
                    if first:
                        P.tt(MIXED[:, j, tg, :], py[:, :], SG.tt[:, k, :], ALU.mult)
                    else:
                        m = MM.next()
                        P.tt(MM.tt[:, m, :], py[:, :], SG.tt[:, k, :], ALU.mult)
                        P.tt(MIXED[:, j, tg, :], MIXED[:, j, tg, :], MM.tt[:, m, :], ALU.add)

    O_MIXED = O_PH + 49152

    def stage_nsa(l):
        MIXED = sb_at("MIXED", [128, 8, 4, 512], BF16, O_MIXED, g=2)
        o = O_PH
        QT = sb_at("QT", [128, 4, 4, 512], BF16, o, g=2); o += 16384
        KWZ = sb_at("KWZ", [128, 2, 4, 512], BF16, O_PH + 90112, g=2)
        KC = sb_at("KC", [128, 128], BF16, o); o += 256
        VCX = sb_at("VCX", [128, 2, 97], BF16, o); o += 512
        VS = sb_at("VS", [128, 16, 2, 65], BF16, o, g=1); o += 4160
        VW = sb_at("VW", [128, 16, 2, 65], BF16, o, g=1); o += 4160
        GT = sb_at("GT", [128, 16, 24], F32, o, g=1); o += 1536
        CBIAS = sb_at("CBIAS", [128, 4], F32, o); o += 64
        assert o <= O_PH + 32768
        o = O_PH + 32768
        OT = sb_at("OT", [128, 4, 4, 512], BF16, o, g=2); o += 16384
        o1 = o
        assert o1 == O_MIXED
        KCV = sb_at("KCV", [128, 2, 2048], BF16, o, g=1); o += 8192
        U2 = sb_at("U2", [128, 2, 2048], BF16, o, g=1); o += 8192
        SQb = sb_at("SQb", [128, 2, 512], BF16, o, g=1); o += 2048
        QG = sb_at("QG", [128, 2, 512], BF16, o, g=1); o += 2048
        LR = sb_at("LR", [128, 4, 512], F32, o, g=1); o += 8192
        T1 = sb_at("T1", [128, 2, 512], F32, o, g=1); o += 4096
        T2 = sb_at("T2", [128, 2, 512], F32, o, g=1); o += 4096
        HID = sb_at("HIDc", [128, 2, 2, 128], BF16, o, g=1); o += 1024
        GL = sb_at("GL", [128, 4, 128], F32, o, g=1); o += 2048
        assert o <= SB_END
        psA = PsRing([0, 1, 2, 3])
        psB = PsRing([4, 5])
        psC = PsRing([6, 7])

        if "no_prmb" not in dbg:
            P.copy(PRMB[:, :], prm(l, P_PE, 32))
        if "no_vcx" not in dbg:
            for g in range(2):
                P.memset(V(VCX.t[:, g, 64:65], VCX[:, :, :].regs), 1.0)
                P.copy(V(VCX.t[:, g, 65:97], VCX[:, :, :].regs), cb(K_OV, 32))
        if "no_ones" not in dbg:
            P.memset(V(KWZ.t[64:128, :, :, :], KWZ.allregs()), 0.0)
            P.memset(V(VS.t[:, :, :, 64:65], VS.allregs()), 1.0)
            P.memset(V(VW.t[:, :, :, 64:65], VW.allregs()), 1.0)

        if "no_wload" not in dbg:
            WV = loadw(w_in[l, :, C_VS:C_VS + 408], 8, 408)
            WQ = loadw(w_in[l, :, C_Q:C_Q + 512], 8, 512)
            WK = loadw(w_in[l, :, C_KC:C_KC + 512], 8, 512)
            W1s = [loadw(cmp_w1[l * 2 + kv, :, :], 16, 256) for kv in range(2)]
            for kv in range(2):
                P.dma(W2C[:, kv, :, :], NR(cmp_w2[l * 2 + kv, :, :].rearrange("(k p) n -> p k n", p=128)), q="pool")
        for i in range(NT if "no_tm" not in dbg else 0):
            tg, off = i // 4, (i % 4) * 128
            ps = psA.next()
            for kc in range(8):
                P.mm(ps[:, 0:408], H[:, kc, tg, off:off + 128], WV(kc, 0, 408), start=(kc == 0), stop=(kc == 7))
            if "no_tmc" in dbg:
                continue
            if "no_tma" not in dbg:
                P.copy(V(VS.t[:, i, :, 0:64], VS.regs_for((0, i))), V(ps.t[:, 0:128].rearrange("p (g d) -> p g d", g=2), ps[:, :].regs), eng="act")
            if "no_tmb" not in dbg:
                P.copy(V(VW.t[:, i, :, 0:64], VW.regs_for((0, i))), V(ps.t[:, 256:384].rearrange("p (g d) -> p g d", g=2), ps[:, :].regs))
            if "no_tms" not in dbg:
                P.act(GT[:, i, :], ps[:, 384:408], AF.Sigmoid)

        if upto == "nsa1a":
            P.barrier()
            return None
        rope_n = [0]

        def rope_a(ps, gcol):
            k = rope_n[0] % 2; rope_n[0] += 1
            P.act(SQb[:, k, :], ps[:, :], AF.Square)
            P.act(QG[:, k, :], ps[:, :], AF.Identity, scale=prm(l, gcol))
            return k

        def rope_b(k, tg, outs):
            pm = psB.next()
            P.mm(pm[:, :], cb(K_B64), SQb[:, k, :])
            pr = psC.next()
            P.mm(pr[:, :], cb(K_ROT), QG[:, k, :])
            P.act(LR[:, 2 * k, :], pm[:, :], AF.Ln, bias=EPS)
            P.act(LR[:, 2 * k + 1, :], LR[:, 2 * k, :], AF.Exp, scale=-0.5)
            P.tt(T1[:, k, :], QG[:, k, :], COS[:, tg, :], ALU.mult, eng="pool")
            P.tt(T2[:, k, :], pr[:, :], SIN[:, tg, :], ALU.mult)
            P.tt(T1[:, k, :], T1[:, k, :], T2[:, k, :], ALU.add)
            for (ov, p0, p1) in outs:
                P.tt(ov, V(T1.t[p0:p1, k, :], T1.regs_for((0, k))), V(LR.t[p0:p1, 2 * k + 1, :], LR.regs_for((0, 2 * k + 1))), ALU.mult,
                     eng="dve")

        jobs = []
        for m in range(4):
            for tg in range(4):
                jobs.append((WQ, m * 128, P_QG, tg, [(QT[:, m, tg, :], 0, 128)]))
        for tg in range(4):
            jobs.append((WK, 0, P_KG + 0, tg, [(V(KCV.t[:, 0, tg * 512:(tg + 1) * 512], KCV.regs_for((0, 0))), 0, 128)]))
            jobs.append((WK, 128, None, tg, [(V(KCV.t[:, 1, tg * 512:(tg + 1) * 512], KCV.regs_for((0, 1))), 0, 128)]))
            jobs.append((WK, 256, P_KG + 1, tg, [(V(KE.t[0:64, 0, tg, :], KE.regs_for((0, 0, tg))), 0, 64),
                                                 (V(KE.t[0:64, 1, tg, :], KE.regs_for((0, 1, tg))), 64, 128)]))
            jobs.append((WV, 128, P_KG + 2, tg, [(V(KWZ.t[0:64, 0, tg, :], KWZ.regs_for((0, 0, tg))), 0, 64),
                                                (V(KWZ.t[0:64, 1, tg, :], KWZ.regs_for((0, 1, tg))), 64, 128)]))
        pend = None
        for (wacc, c0, gcol, tg, outs) in jobs:
            ps = psA.next()
            proj_fm(ps, wacc, c0, 8, H, tg)
            if gcol is None:
                P.copy(outs[0][0], ps[:, :], eng="act")
                continue
            k = rope_a(ps, gcol)
            if pend is not None:
                rope_b(*pend)
            pend = (k, tg, outs)
        rope_b(*pend)

        if upto == "nsa1b":
            P.barrier()
            return None
        for kv in range(2):
            W1 = W1s[kv]

            def W2(nch, c0, c1, kv=kv):
                return V(W2C.t[:, kv, nch, c0:c1], W2C.regs_for((0, kv)))
            for nch in range(2):
                ps = psB.next()
                for l2 in range(16):
                    P.mm(ps[:, 0:1], W1(l2, nch * 128, nch * 128 + 128), PRMB[:, kv * 16 + l2: kv * 16 + l2 + 1],
                         start=(l2 == 0), stop=(l2 == 15))
                P.tt(CBIAS[:, kv * 2 + nch: kv * 2 + nch + 1], ps[:, 0:1], prm(l, P_B1 + kv * 2 + nch), ALU.add)
            for g in range(2):
                ub = (kv * 2 + g) % 2
                u2r = U2.regs_for((0, ub))
                P.copy(V(U2.t[0:64, ub, :], u2r), V(KCV.t[64 * g:64 * g + 64, kv, :], KCV.regs_for((0, kv))))
                P.copy(V(U2.t[64:128, ub, 0:2047], u2r), V(KCV.t[64 * g:64 * g + 64, kv, 1:2048], KCV.regs_for((0, kv))), eng="act")
                P.memset(V(U2.t[64:128, ub, 2047:2048], u2r), 0.0)
                for nch in range(2):
                    ps = psA.next()
                    for l2 in range(16):
                        rhs = V(bass.AP(U2.t, ub * 2048 + 2 * l2, [[2 * 2048, 128], [16, 127]]), u2r)
                        P.mm(ps[:, 0:127], W1(l2, nch * 128, nch * 128 + 128), rhs, start=(l2 == 0), stop=(l2 == 15))
                    xh = GL[:, 0, 0:127]; x2 = GL[:, 1, 0:127]; x3 = GL[:, 2, 0:127]; sg = GL[:, 3, 0:127]
                    P.act(xh, ps[:, 0:127], AF.Identity, bias=CBIAS[:, kv * 2 + nch: kv * 2 + nch + 1])
                    P.tt(x2, xh, xh, ALU.mult)
                    P.tt(x3, x2, xh, ALU.mult)
                    P.stt(x2, x3, 0.044715, xh, ALU.mult, ALU.add)
                    P.act(sg, x2, AF.Sigmoid, scale=1.5957691216057308)
                    P.tt(HID[:, g, nch, 0:127], xh, sg, ALU.mult)
                ps2 = psC.next()
                if kv == 0:
                    for nch in range(2):
                        P.mm(ps2[0:64, 0:127], W2(nch, 0, 64), HID[:, g, nch, 0:127], start=(nch == 0), stop=(nch == 1))
                    P.copy(V(KC.t[64 * g:64 * g + 64, 0:127], KC[:, :].regs), ps2[0:64, 0:127])
                else:
                    for nch in range(2):
                        P.mm(ps2[0:127, 0:64], HID[:, g, nch, 0:127], W2(nch, 0, 64), start=(nch == 0), stop=(nch == 1))
                    P.copy(V(VCX.t[0:127, g, 0:64], VCX[:, :, :].regs), ps2[0:127, 0:64])

        if "q" in taps:
            tq = nc.dram_tensor("tap_q", [128, 4 * 2048], BF16, kind="ExternalOutput"); tap_out["q"] = tq
            P.dma(NR(tq[:, :]), V(QT.t[:, :, :, :].rearrange("p a b c -> p (a b c)"), QT.allregs()))
            tk = nc.dram_tensor("tap_ke", [128, 2 * 2048], BF16, kind="ExternalOutput"); tap_out["ke"] = tk
            P.dma(NR(tk[0:96, :]), V(KE.t[0:96, :, :, :].rearrange("p a b c -> p (a b c)"), KE.allregs()))
            tk = nc.dram_tensor("tap_kc", [128, 128], BF16, kind="ExternalOutput"); tap_out["kc"] = tk
            P.dma(NR(tk[:, 0:127]), V(KC.t[:, 0:127], KC[:, :].regs))
            tk = nc.dram_tensor("tap_vcx", [128, 2 * 97], BF16, kind="ExternalOutput"); tap_out["vcx"] = tk
            P.dma(NR(tk[0:127, :]), V(VCX.t[0:127, :, :].rearrange("p a b -> p (a b)"), VCX[:, :, :].regs))
            tk = nc.dram_tensor("tap_vs", [128, 16 * 130], BF16, kind="ExternalOutput"); tap_out["vs"] = tk
            P.dma(NR(tk[:, :]), V(VS.t[:, :, :, :].rearrange("p a b c -> p (a b c)"), VS.allregs()))
            tk = nc.dram_tensor("tap_gt", [128, 16 * 24], F32, kind="ExternalOutput"); tap_out["gt"] = tk
            P.dma(NR(tk[:, :]), V(GT.t[:, :, :].rearrange("p a b -> p (a b)"), GT.allregs()))

        P.barrier()
        if upto == "nsa1":
            return None
        o = o1
        ET = sb_at("ET", [128, 24, 512], BF16, o, g=1); o += 24576
        QSEL = sb_at("QSEL", [128, 4, 512], BF16, o, g=1); o += 4096
        OTL = sb_at("OTL", [128, 2, 512], F32, o, g=1); o += 4096
        OB = sb_at("OB", [128, 2, 256], F32, o, g=1); o += 2048
        SM = sb_at("SM", [128, 8, 8], F32, o, g=1); o += 256
        IM = sb_at("IM", [128, 4, 128], F32, o, g=1); o += 2048
        SC = sb_at("SC", [128, 4, 96], F32, o, g=1); o += 1536
        M8 = sb_at("M8", [128, 4, 16], F32, o, g=1); o += 256
        assert o <= SB_END
        P.memset(V(QSEL.t[64:128, :, :], QSEL.allregs()), 0.0)
        psS = PsRing([0, 1, 2])
        psAcc = PsRing([3, 4, 5])
        psM = PsRing([6, 7])
        eti = [0]
        smi = [0]

        def q4(i, g):
            tg, off = i // 4, (i % 4) * 128
            return V(QT.t[64 * g:64 * g + 64, :, tg, off:off + 128], QT.regs_for((0, slice(None), tg)))

        def norm_acc(acc, i, g, branch, ncol):
            k = smi[0] % 8; smi[0] += 1
            a4 = acc.t[:, 0:4 * ncol].rearrange("p (r c) -> p r c", r=4)
            den = V(a4[:, :, 64], acc[:, :].regs)
            rd = V(SM.t[:, k, 0:4], SM.regs_for((0, k)))
            cf = V(SM.t[:, k, 4:8], SM.regs_for((0, k)))
            P.ts(rd, den, 1e-30, ALU.max)
            P.add("dve", lambda e: e.reciprocal(rd.ap, rd.ap), reads=[rd], writes=[rd])
            gv = V(bass.AP(GT.t, i * 24 + g * 12 + branch, [[16 * 24, 128], [3, 4]]), GT.regs_for((0, i)))
            P.tt(cf, rd, gv, ALU.mult)
            cfb = V(bass.AP(SM.t, k * 8 + 4, [[64, 128], [1, 4], [0, 64]]), SM.regs_for((0, k)))
            num = V(a4[:, :, 0:64], acc[:, :].regs)
            ok = i % 2
            dst = V(OTL.t[:, ok, g * 256:(g + 1) * 256].rearrange("p (r d) -> p r d", r=4), OTL.regs_for((0, ok)))
            if branch == 0:
                P.tt(dst, num, cfb, ALU.mult)
            else:
                kb = (i * 2 + g) % 2
                ob = V(OB.t[:, kb, :].rearrange("p (r d) -> p r d", r=4), OB.regs_for((0, kb)))
                P.tt(ob, num, cfb, ALU.mult)
                P.tt(dst, dst, ob, ALU.add)
            return rd, k

        def S_cmp(i, g):
            sc = psS.next()
            P.mm(sc[0:127, :], V(KC.t[64 * g:64 * g + 64, 0:127], KC[:, :].regs), q4(i, g))
            ek = eti[0] % 24; eti[0] += 1
            e = ET[0:127, ek, :]
            P.act(e, sc[0:127, :], AF.Exp, scale=0.125)
            e3 = V(ET.t[0:127, ek, :].rearrange("p (r q) -> p r q", r=4), ET.regs_for((0, ek)))
            msk = V(bass.AP(CM.t, i * 128, [[2048, 127], [0, 4], [1, 128]]), CM[:, :].regs)
            P.tt(e3, e3, msk, ALU.mult)
            qs = (i % 2) * 2 + g
            qsr = QSEL.regs_for((0, qs))
            P.copy(V(QSEL.t[0:64, qs, :].rearrange("p (r q) -> p r q", r=4), qsr), q4(i, g), eng="act")
            return dict(kind="cmp", i=i, g=g, ek=ek, qs=qs)

        def V_cmp(c):
            i, g, ek, qs = c["i"], c["g"], c["ek"], c["qs"]
            acc = psAcc.next()
            for r in range(4):
                P.mm(acc[:, r * 97:(r + 1) * 97], V(ET.t[0:127, ek, r * 128:(r + 1) * 128], ET.regs_for((0, ek))),
                     V(VCX.t[0:127, g, :], VCX[:, :, :].regs))
            rd, k = norm_acc(acc, i, g, 0, 97)
            if i >= 8:
                a4 = acc.t[:, 0:388].rearrange("p (r c) -> p r c", r=4)
                kk = qs
                im = V(IM.t[:, kk, :].rearrange("p (r j) -> p r j", r=4), IM.regs_for((0, kk)))
                rdb = V(bass.AP(SM.t, k * 8, [[64, 128], [1, 4], [0, 32]]), SM.regs_for((0, k)))
                P.tt(im, V(a4[:, :, 65:97], acc[:, :].regs), rdb, ALU.mult)
                imr = V(bass.AP(IM.t, kk * 128, [[512, 128], [1, 32], [32, 4]]), IM.regs_for((0, kk)))
                scr = SC.regs_for((0, kk))
                score = V(SC.t[:, kk, 0:32], scr); s2 = V(SC.t[:, kk, 32:64], scr); sn = V(SC.t[:, kk, 64:96], scr)
                P.add("dve", lambda e_: e_.tensor_reduce(score.ap, imr.ap, AX.X, ALU.add), reads=[imr], writes=[score])
                P.tt(score, score, V(CF.t[:, 128 + (i - 8) * 32: 128 + (i - 7) * 32], CF[:, :].regs), ALU.add)
                m8r = M8.regs_for((0, kk))
                ma = V(M8.t[:, kk, 0:8], m8r); mb = V(M8.t[:, kk, 8:16], m8r)
                P.add("dve", lambda e_: e_.max(ma.ap, score.ap), reads=[score], writes=[ma])
                P.add("dve", lambda e_: e_.match_replace(s2.ap, ma.ap, score.ap, -1e9), reads=[score, ma], writes=[s2])
                P.add("dve", lambda e_: e_.max(mb.ap, s2.ap), reads=[s2], writes=[mb])
                P.ts(sn, score, V(M8.t[:, kk, 15:16], m8r), ALU.is_lt, -BIGSEL, ALU.mult)
                c["sn"] = sn

        def T_cmp(c):
            if c["i"] < 8:
                return
            qs = c["qs"]
            qsr = QSEL.regs_for((0, qs))
            pt = psM.next()
            P.transpose(pt[0:32, 0:128], c["sn"], IDF)
            src = V(bass.AP(pt.t, 0, [[512, 32], [0, 4], [1, 128]]), pt[:, :].regs)
            dst = V(bass.AP(QSEL.t, 64 * 2048 + qs * 512, [[2048, 32], [128, 4], [1, 128]]), qsr)
            P.copy(dst, src)

        def S_keys(i, g, branch):
            if branch == 1:
                js = list(range(0, i + 1))
            else:
                js = list(range(max(0, i - 4), i + 1))
            qs = (i % 2) * 2 + g
            es = []
            for j in js:
                sc = psS.next()
                jt, joff = j // 4, (j % 4) * 128
                if branch == 1:
                    P.mm(sc[:, :], V(KE.t[:, g, jt, joff:joff + 128], KE.regs_for((0, g, jt))),
                         V(QSEL.t[:, qs, :], QSEL.regs_for((0, qs))))
                else:
                    P.mm(sc[:, :], V(KWZ.t[:, g, jt, joff:joff + 128], KWZ.regs_for((0, g, jt))),
                         V(QSEL.t[:, qs, :], QSEL.regs_for((0, qs))))
                ek = eti[0] % 24; eti[0] += 1
                P.act(ET[:, ek, :], sc[:, :], AF.Exp, scale=0.125)
                if j == i:
                    P.tt(ET[:, ek, :], ET[:, ek, :], cb(K_DIAG, 512), ALU.mult)
                elif branch == 2 and j == i - 4:
                    P.tt(ET[:, ek, :], ET[:, ek, :], cb(K_LOW, 512), ALU.mult)
                es.append(ek)
            return dict(kind="keys", i=i, g=g, branch=branch, js=js, es=es)

        def V_keys(c):
            i, g, branch, js, es = c["i"], c["g"], c["branch"], c["js"], c["es"]
            acc = psAcc.next()
            VV = VS if branch == 1 else VW
            for r in range(4):
                for n, j in enumerate(js):
                    P.mm(acc[:, r * 65:(r + 1) * 65], V(ET.t[:, es[n], r * 128:(r + 1) * 128], ET.regs_for((0, es[n]))),
                         V(VV.t[:, j, g, :], VV.regs_for((0, j))), start=(n == 0), stop=(n == len(js) - 1))
            norm_acc(acc, i, g, branch, 65)

        def finish_tile(i):
            ok = i % 2
            tg, off = i // 4, (i % 4) * 128
            pt = psM.next()
            for c in range(4):
                P.transpose(pt[:, c * 128:(c + 1) * 128], OTL[:, ok, c * 128:(c + 1) * 128], IDF)
            P.copy(V(OT.t[:, :, tg, off:off + 128], OT.regs_for((0, slice(None), tg))),
                   V(pt.t[:, :].rearrange("p (c q) -> p c q", c=4), pt[:, :].regs), eng="act")

        units = [("cmp", 0, 0), ("cmp", 0, 1)]
        for i in range(NT):
            for g in range(2):
                units.append(("win", i, g))
                if i + 1 < NT:
                    units.append(("cmp", i + 1, g))
                units.append(("slc", i, g))
            units.append(("fin", i, 0))

        def S_unit(u):
            kind, i, g = u
            if kind == "cmp":
                return S_cmp(i, g)
            if kind == "win":
                return S_keys(i, g, 2)
            if kind == "slc":
                return S_keys(i, g, 1)
            return dict(kind="fin", i=i)

        def V_unit(c):
            if c["kind"] == "cmp":
                V_cmp(c)
            elif c["kind"] == "keys":
                V_keys(c)

        ctxs = [None] * len(units)
        ctxs[0] = S_unit(units[0])
        deferred = []
        for k in range(len(units)):
            if k + 1 < len(units):
                if units[k + 1][0] == "slc":
                    for d in [d for d in deferred if d[2] == ("T", units[k + 1][1], units[k + 1][2])]:
                        d[1](); deferred.remove(d)
                ctxs[k + 1] = S_unit(units[k + 1])
            c = ctxs[k]
            V_unit(c)
            for d in [d for d in deferred if d[0] <= k]:
                d[1](); deferred.remove(d)
            if c["kind"] == "cmp" and c["i"] >= 8:
                deferred.append((k + 2, (lambda cc=c: T_cmp(cc)), ("T", c["i"], c["g"])))
            if c["kind"] == "fin":
                deferred.append((k + 1, (lambda ii=c["i"]: finish_tile(ii)), ("F", c["i"], 0)))
        for d in deferred:
            d[1]()

        if "ot" in taps:
            tk = nc.dram_tensor("tap_ot", [128, 4 * 2048], BF16, kind="ExternalOutput"); tap_out["ot"] = tk
            P.dma(NR(tk[:, :]), V(OT.t[:, :, :, :].rearrange("p a b c -> p (a b c)"), OT.allregs()))
        P.barrier()
        if upto == "nsa2":
            return None
        o = O_MIXED + 32768
        SG = Ring(sb_at("SG", [128, 2, 512], F32, o, g=1), 2); o += 4096
        MM = Ring(sb_at("MMt", [128, 2, 512], F32, o, g=1), 2); o += 4096
        out_branch(l, OT, lambda kc, tg: OT[:, kc, tg, :], nsa_w_out, C_GC, True, MIXED, SG, MM, PsRing([0, 1, 2, 3, 4, 5]))
        P.barrier()
        return MIXED

    def stage_a(l, MIXED):
        o = O_PH
        CH = sb_at("CH", [128, 2 + 2048], F32, o); o += 8256
        VA = sb_at("VA", [128, 4, 4, 512], BF16, o, g=2); o += 16384
        CSB = sb_at("CSB", [128, 2, 512], F32, o, g=1); o += 4096
        VT = sb_at("VT", [128, 2, 512], F32, o, g=1); o += 4096
        assert o <= O_MIXED
        o = O_MIXED + 32768
        SG = Ring(sb_at("SG", [128, 2, 512], F32, o, g=1), 2); o += 4096
        MM = Ring(sb_at("MMt", [128, 2, 512], F32, o, g=1), 2); o += 4096
        psr = PsRing([0, 1, 2, 3, 4, 5])
        WC = loadw(w_in[l, :, C_AC:C_AC + 512], 8, 512)
        WH = loadw(w_in[l, :, C_AH:C_AH + 512], 8, 512)
        WB = loadw(w_in[l, :, C_AB:C_AB + 512], 8, 512)
        chr_ = CH[:, :].regs
        n = 0
        for c in range(4):
            P.memset(V(CH.t[:, 0:2], chr_), 0.0)
            for tg in range(4):
                pc = psr.next(); proj_fm(pc, WC, c * 128, 8, H, tg)
                ph = psr.next(); proj_fm(ph, WH, c * 128, 8, H, tg)
                pb = psr.next(); proj_fm(pb, WB, c * 128, 8, H, tg)
                k = n % 2; n += 1
                P.copy(CSB[:, k, :], pc[:, :], eng="act")
                t0 = 2 + tg * 512
                P.tt(V(CH.t[:, t0:t0 + 512], chr_), CSB[:, k, :], ph[:, :], ALU.mult)
                wc = P_ACW + c * 3
                P.ts(VT[:, k, :], V(CH.t[:, t0:t0 + 512], chr_), prm(l, wc + 2), ALU.mult)
                P.stt(VT[:, k, :], V(CH.t[:, t0 - 1:t0 + 511], chr_), prm(l, wc + 1), VT[:, k, :], ALU.mult, ALU.add)
                P.stt(VT[:, k, :], V(CH.t[:, t0 - 2:t0 + 510], chr_), prm(l, wc + 0), VT[:, k, :], ALU.mult, ALU.add)
                P.tt(VA[:, c, tg, :], VT[:, k, :], pb[:, :], ALU.mult)
        if "va" in taps:
            tk = nc.dram_tensor("tap_va", [128, 4 * 2048], BF16, kind="ExternalOutput"); tap_out["va"] = tk
            P.dma(NR(tk[:, :]), V(VA.t[:, :, :, :].rearrange("p a b c -> p (a b c)"), VA.allregs()))
        out_branch(l, VA, lambda kc, tg: VA[:, kc, tg, :], a_w_out, C_GA, False, MIXED, SG, MM, psr)
        P.barrier()

    def stage_b(l, MIXED):
        o = O_PH
        U = sb_at("U", [128, 4, 2080], BF16, o, g=1); o += 16640
        UR = {(c, b): Reg(f"U{c}_{b}") for c in range(4) for b in range(5)}

        def ureg(c, a, b):
            return [UR[(c, k)] for k in range(a // 512, (b - 1) // 512 + 1)]
        DG = sb_at("DG", [128, 4, 31, 128], BF16, o, g=2); o += 31744
        assert o <= O_MIXED
        o = O_MIXED + 32768
        SG = Ring(sb_at("SG", [128, 2, 512], F32, o, g=1), 2); o += 4096
        MM = Ring(sb_at("MMt", [128, 2, 512], F32, o, g=1), 2); o += 4096
        SQ = sb_at("SQ", [128, 2, 512], BF16, o, g=1); o += 2048
        LR = sb_at("LRb", [128, 4, 512], F32, o, g=1); o += 8192
        assert o <= SB_END, o
        psr = PsRing([0, 1, 2, 3])
        psS = PsRing([4, 5, 6, 7])
        WA_ = loadw(w_in[l, :, C_BA:C_BA + 512], 8, 512)
        WG_ = loadw(w_in[l, :, C_BG:C_BG + 512], 8, 512)
        for c in range(4):
            for k in range(31):
                dgv = DG[:, c, k, :]; idv = cb(K_ID); wv = prm(l, P_BCW + c * 31 + k)
                P.add("pool", (lambda e, a=dgv, b=idv, w=wv: e.tensor_scalar(a.ap, b.ap, w.ap, 0.0, ALU.mult, op1=ALU.add)),
                      reads=[idv, wv, V(None, [P.tok])], writes=[dgv])
        for c in range(4):
            P.memset(V(U.t[:, c, 0:30], ureg(c, 0, 30)), 0.0)
            for tg in range(4):
                pa = psr.next(); proj_fm(pa, WA_, c * 128, 8, H, tg)
                pg = psr.next(); proj_fm(pg, WG_, c * 128, 8, H, tg)
                k = SG.next()
                P.act(SG.tt[:, k, :], pg[:, :], AF.Sigmoid)
                P.tt(V(U.t[:, c, 30 + tg * 512: 30 + (tg + 1) * 512], ureg(c, 30 + tg * 512, 30 + (tg + 1) * 512)), pa[:, :], SG.tt[:, k, :], ALU.mult)
        n = 0
        pend = [None]

        def flush_stats():
            if pend[0] is not None:
                pm_, pq_, uc_, sq_, c_ = pend[0]
                P.mm(pm_[:, :], cb(K_O512), uc_, start=(c_ == 0), stop=(c_ == 3))
                P.mm(pq_[:, :], cb(K_O512), sq_, start=(c_ == 0), stop=(c_ == 3))
                pend[0] = None

        def ln_tail(tg, pmean, pmsq):
            P.act(LR[:, 0, :], pmean[:, :], AF.Square)
            P.tt(LR[:, 1, :], pmsq[:, :], LR[:, 0, :], ALU.subtract)
            P.act(LR[:, 2, :], LR[:, 1, :], AF.Ln, bias=EPS)
            P.act(LR[:, 3, :], LR[:, 2, :], AF.Exp, scale=-0.5)
            for c in range(4):
                uc = V(U.t[:, c, tg * 512:(tg + 1) * 512], ureg(c, tg * 512, (tg + 1) * 512))
                m = MM.next()
                P.tt(MM.tt[:, m, :], uc, pmean[:, :], ALU.subtract)
                P.tt(MM.tt[:, m, :], MM.tt[:, m, :], LR[:, 3, :], ALU.mult)
                P.act(uc, MM.tt[:, m, :], AF.Silu, bias=prm(l, P_BLB + c), scale=prm(l, P_BLG + c))
        tails = []
        for tg in range(4):
            pmean = psS.next()
            pmsq = psS.next()
            for c in range(4):
                pcv = psr.next()
                for k in range(31):
                    P.mm(pcv[:, :], DG[:, c, k, :],
                         V(U.t[:, c, tg * 512 + k: tg * 512 + k + 512], ureg(c, tg * 512 + k, tg * 512 + k + 512)), start=(k == 0), stop=(k == 30))
                flush_stats()
                if tails:
                    ln_tail(*tails.pop())
                uc = V(U.t[:, c, tg * 512:(tg + 1) * 512], ureg(c, tg * 512, (tg + 1) * 512))
                kk = n % 2; n += 1
                P.act(SQ[:, kk, :], pcv[:, :], AF.Square, bias=prm(l, P_BCB + c))
                P.act(uc, pcv[:, :], AF.Identity, bias=prm(l, P_BCB + c))
                pend[0] = (pmean, pmsq, uc, SQ[:, kk, :], c)
            tails.append((tg, pmean, pmsq))
        flush_stats()
        ln_tail(*tails.pop())
        if "ub" in taps:
            tk = nc.dram_tensor("tap_ub", [128, 4, 2080], BF16, kind="ExternalOutput"); tap_out["ub"] = tk
            P.dma(NR(tk[:, :, 0:2048]), V(U.t[:, :, 0:2048], list(UR.values())))
        out_branch(l, U, lambda kc, tg: V(U.t[:, kc, tg * 512:(tg + 1) * 512], ureg(kc, tg * 512, (tg + 1) * 512)), b_w_out, C_GB, False,
                   MIXED, SG, MM, PsRing([0, 1, 2, 3, 4, 5]))
        P.barrier()

    def stage_wo(l, MIXED):
        o = O_PH
        XT = Ring(sb_at("wXT", [128, 6, 512], F32, o, g=1), 6); o += 12288
        psr = PsRing([0, 1, 2, 3, 4, 5])
        if "mixed" in taps:
            tk = nc.dram_tensor("tap_mixed", [128, 8 * 2048], BF16, kind="ExternalOutput"); tap_out["mixed"] = tk
            P.dma(NR(tk[:, :]), V(MIXED.t[:, :, :, :].rearrange("p a b c -> p (a b c)"), MIXED.allregs()))
        its = [(jg, jj, tg) for jg in range(2) for jj in range(4) for tg in range(4)]
        PF = 3
        slots = {}
        psr = PsRing([0, 1, 2, 3])
        SQw = Ring(sb_at("wSQ", [128, 3, 512], BF16, o, g=1), 3); o += 3072
        pend_stats = []

        def xload(n):
            jg, jj, tg = its[n]
            k = XT.next()
            slots[n] = k
            P.dma(XT.tt[:, k, :], Xsrc(jg * 4 + jj, tg), q="sp")
        for n in range(min(PF, len(its))):
            xload(n)
        WO = None
        for n, (jg, jj, tg) in enumerate(its):
            if jj == 0 and tg == 0:
                WO = loadw(w_o[l, :, jg * 512:(jg + 1) * 512], 8, 512)
            j = jg * 4 + jj
            ps = psr.next()
            proj_fm(ps, WO, jj * 128, 8, MIXED, tg)
            if n + PF < len(its):
                xload(n + PF)
            k = slots[n]
            P.tt(XT.tt[:, k, :], XT.tt[:, k, :], ps[:, :], ALU.add)
            P.dma(Xdst(j, tg), XT.tt[:, k, :], q="sp")
            q_ = SQw.next()
            P.act(SQw.tt[:, q_, :], XT.tt[:, k, :], AF.Square)
            while len(pend_stats) >= 2:
                pend_stats.pop(0)()
            pend_stats.append(lambda q_=q_, j=j, tg=tg: P.mm(PSB[4 + tg][:, :], cb(K_O1024), SQw.tt[:, q_, :],
                                                             start=(j == 0), stop=(j == 7)))
        for f in pend_stats:
            f()
        xstate["in_y"] = True

    def stage_ffn(l, want_stats):
        o = O_PH
        HID = sb_at("HID", [128, 22, 4, 512], BF16, o, g=2); o += 90112
        SU = Ring(sb_at("SU", [128, 2, 512], F32, o, g=1), 2)
        XT = Ring(sb_at("fXT", [128, 4, 512], F32, o, g=1), 4); o += 10240
        assert o <= SB_END
        psr = PsRing([0, 1, 2, 3, 4, 5])
        for hg in range(6):
            nc_ = min(512, 2816 - hg * 512)
            W1 = loadw(ffn_w13[l, :, hg * 512: hg * 512 + nc_], 8, nc_)
            W3 = loadw(ffn_w13[l, :, 2816 + hg * 512: 2816 + hg * 512 + nc_], 8, nc_)
            for jj in range(nc_ // 128):
                hc = hg * 4 + jj
                for tg in range(4):
                    pu = psr.next(); proj_fm(pu, W1, jj * 128, 8, H, tg)
                    pv = psr.next(); proj_fm(pv, W3, jj * 128, 8, H, tg)
                    k = SU.next()
                    P.act(SU.tt[:, k, :], pu[:, :], AF.Silu)
                    P.tt(HID[:, hc, tg, :], SU.tt[:, k, :], pv[:, :], ALU.mult)
        P.barrier()
        its = [(jo, tg) for jo in range(8) for tg in range(4)]
        PF = 3
        slots = {}
        psr = PsRing([0, 1, 2, 3])
        SQf = Ring(sb_at("fSQ", [128, 2, 512], BF16, O_PH + 90112 + 8192, g=1), 2)
        pend_stats = []

        def xload(n):
            jo, tg = its[n]
            k = XT.next()
            slots[n] = k
            P.dma(XT.tt[:, k, :], Xsrc(jo, tg), q="sp")
        for n in range(PF):
            xload(n)
        W2 = None
        for n, (jo, tg) in enumerate(its):
            if tg == 0:
                W2 = loadw(ffn_w2[l, :, jo * 128:(jo + 1) * 128], 22, 128)
            ps = psr.next()
            for kc in range(22):
                P.mm(ps[:, :], W2(kc, 0, 128), HID[:, kc, tg, :], start=(kc == 0), stop=(kc == 21))
            if n + PF < len(its):
                xload(n + PF)
            k = slots[n]
            P.tt(XT.tt[:, k, :], XT.tt[:, k, :], ps[:, :], ALU.add)
            P.dma(Xdst(jo, tg), XT.tt[:, k, :], q="sp")
            if want_stats:
                q_ = SQf.next()
                P.act(SQf.tt[:, q_, :], XT.tt[:, k, :], AF.Square)
                while len(pend_stats) >= 1:
                    pend_stats.pop(0)()
                pend_stats.append(lambda q_=q_, jo=jo, tg=tg: P.mm(PSB[4 + tg][:, :], cb(K_O1024), SQf.tt[:, q_, :],
                                                                   start=(jo == 0), stop=(jo == 7)))
        for f in pend_stats:
            f()
        P.barrier()

    P.barrier()
    for l in range(n_layers):
        if l == 0:
            stage_norm(l, P_N1)
        else:
            stage_norm_apply(l, P_N1)
        if "h" in taps and l == 0:
            tk = nc.dram_tensor("tap_h", [128, 8 * 2048], BF16, kind="ExternalOutput"); tap_out["h"] = tk
            P.dma(NR(tk[:, :]), V(H.t[:, :, :, :].rearrange("p a b c -> p (a b c)"), H.allregs()))
        P.barrier()
        if upto == "norm":
            break
        MIXED = stage_nsa(l)
        if upto in ("nsa1", "nsa2", "nsa3", "nsa1a", "nsa1b"):
            break
        stage_a(l, MIXED)
        if upto == "a":
            break
        stage_b(l, MIXED)
        if upto == "b":
            break
        stage_wo(l, MIXED)
        if upto == "wo":
            break
        stage_norm_apply(l, P_N2)
        P.barrier()
        stage_ffn(l, l + 1 < n_layers)
    P.emit()
    return nc, P, tap_out


_CACHE = {}


def kernel(**inputs):
    shared, xs = _prep(inputs)
    if "nc" not in _CACHE:
        _CACHE["nc"] = build()[0]
    nc = _CACHE["nc"]
    in_maps = []
    for b in range(8):
        m = dict(shared)
        m["xT"] = xs[b]
        in_maps.append(m)
    res = run_bass_kernel_spmd(nc, in_maps, core_ids=list(range(8)))
    out = np.stack([np.ascontiguousarray(np.asarray(r["yT"]).T) for r in res.results], axis=0)
    return out.astype(np.float32)
```
